# Optimizing a Trainium2 kernel written in Bass

```python
import math
import jax, jax.numpy as jnp
from jax import lax
import numpy as np

D_MODEL = 2048
BATCH = 8
SEQ = 4096
DEPTH = 1

EPS = 1e-6
CONV_WIDTH = D_MODEL // 2
CONV_K = 3
M_HEADS = 8
QK_DIM = D_MODEL // 16
V_DIM = D_MODEL // 8
M_QK = M_HEADS * QK_DIM
M_V = M_HEADS * V_DIM
CHUNK = 64
N_BRANCH = 2
FFN_HIDDEN = int(math.ceil((8 * D_MODEL / 3) / 256) * 256)
SPLITS = (CONV_WIDTH, CONV_WIDTH, CONV_WIDTH, M_QK, M_QK, M_V, M_V, 4 * M_HEADS, N_BRANCH * D_MODEL)
IN_COLS = sum(SPLITS)
SPLIT_IDX = tuple(int(v) for v in np.cumsum(SPLITS)[:-1])

kernel_name = "hybrid_conv_mlstm_gated_merge_encoder"


def rmsnorm(x, w):
    xf = x.astype(jnp.float32)
    y = xf * lax.rsqrt(jnp.mean(xf * xf, axis=-1, keepdims=True) + EPS)
    return (y * w.astype(jnp.float32)).astype(x.dtype)


def modulate(h, shift, scale):
    return h * (1.0 + scale[:, None, :]) + shift[:, None, :]


def short_conv_centred(u, w):
    p = jnp.pad(u, ((0, 0), (1, 1), (0, 0)))
    return w[0] * p[:, :-2] + w[1] * p[:, 1:-1] + w[2] * p[:, 2:]


def mlstm_chunk_step(carry, inp):
    C, n, m = carry
    q, k, v, ig, lf = inp
    L = q.shape[-2]
    tril = jnp.tril(jnp.ones((L, L), dtype=bool))
    b = jnp.cumsum(lf, axis=-1)
    d_log = b[..., :, None] - b[..., None, :] + ig[..., None, :]
    d_log = jnp.where(tril, d_log, -jnp.inf)
    m_inter = b + m[..., None]
    m_t = jnp.maximum(m_inter, jnp.max(d_log, axis=-1))
    s = jnp.einsum('bhjd,bhsd->bhjs', q, k) * jnp.exp(d_log - m_t[..., None])
    inter = jnp.exp(m_inter - m_t)
    num = jnp.einsum('bhjs,bhsv->bhjv', s, v) + inter[..., None] * jnp.einsum('bhjd,bhdv->bhjv', q, C)
    den = jnp.sum(s, axis=-1) + inter * jnp.einsum('bhjd,bhd->bhj', q, n)
    h = num / jnp.maximum(jnp.abs(den), jnp.exp(-m_t))[..., None]
    b_last = b[..., -1]
    w_log = b_last[..., None] - b + ig
    m_new = jnp.maximum(b_last + m, jnp.max(w_log, axis=-1))
    w = jnp.exp(w_log - m_new[..., None])
    decay = jnp.exp(b_last + m - m_new)
    C_new = decay[..., None, None] * C + jnp.einsum('bhs,bhsd,bhsv->bhdv', w, k, v)
    n_new = decay[..., None] * n + jnp.einsum('bhs,bhsd->bhd', w, k)
    return (C_new, n_new, m_new), h


def mlstm_scan(q, k, v, ig, lf):
    Bn, H, S, dk = q.shape
    dv = v.shape[-1]
    nc = S // CHUNK

    def chunks(t):
        return jnp.moveaxis(t.reshape(t.shape[:2] + (nc, CHUNK) + t.shape[3:]), 2, 0)

    init = (jnp.zeros((Bn, H, dk, dv), jnp.float32),
            jnp.zeros((Bn, H, dk), jnp.float32),
            jnp.zeros((Bn, H), jnp.float32))
    _, hs = lax.scan(mlstm_chunk_step, init, (chunks(q), chunks(k), chunks(v), chunks(ig), chunks(lf)))
    return jnp.moveaxis(hs, 0, 2).reshape(Bn, H, S, dv)


def to_heads(t, d):
    Bn, S, _ = t.shape
    return t.reshape(Bn, S, M_HEADS, d).transpose(0, 2, 1, 3).astype(jnp.float32)


def mixer(h, w_in_mix, conv_w, mlstm_gate_bias, mlstm_norm_w, w_conv_out, w_mlstm_out, w_o):
    Bn, S, _ = h.shape
    proj = jnp.einsum('bsd,de->bse', h, w_in_mix)
    cb, cc, cx, q, k, v, o, gpre, bgate = jnp.split(proj, SPLIT_IDX, axis=-1)

    y_conv = jnp.einsum('bsc,cd->bsd', cb * short_conv_centred(cc * cx, conv_w), w_conv_out)

    qh = to_heads(q, QK_DIM)
    kh = to_heads(k, QK_DIM) * (QK_DIM ** -0.5)
    vh = to_heads(v, V_DIM)
    g = (gpre + mlstm_gate_bias).astype(jnp.float32).reshape(Bn, S, 4, M_HEADS).transpose(2, 0, 3, 1)
    i_fwd, f_fwd, i_bwd, f_bwd = g[0], g[1], g[2], g[3]
    h_fwd = mlstm_scan(qh, kh, vh, i_fwd, jax.nn.log_sigmoid(f_fwd))
    flip = lambda t: jnp.flip(t, axis=2)
    h_bwd = flip(mlstm_scan(flip(qh), flip(kh), flip(vh), flip(i_bwd), flip(jax.nn.log_sigmoid(f_bwd))))
    hm = h_fwd + h_bwd
    hm = hm * lax.rsqrt(jnp.mean(hm * hm, axis=-1, keepdims=True) + EPS)
    hm = hm * mlstm_norm_w.astype(jnp.float32).reshape(M_HEADS, 1, V_DIM)
    hm = hm.transpose(0, 2, 1, 3).reshape(Bn, S, M_V).astype(h.dtype)
    hm = jax.nn.sigmoid(o) * hm
    y_mlstm = jnp.einsum('bsv,vd->bsd', hm, w_mlstm_out)

    g_conv, g_mlstm = jnp.split(jax.nn.sigmoid(bgate), 2, axis=-1)
    merged = g_conv * y_conv + g_mlstm * y_mlstm
    return jnp.einsum('bsd,de->bse', merged, w_o)


def swiglu(h, w_gate_up, w_down):
    gu = jnp.einsum('bsd,df->bsf', h, w_gate_up)
    gt, up = jnp.split(gu, 2, axis=-1)
    return jnp.einsum('bsf,fd->bsd', jax.nn.silu(gt) * up, w_down)


def setup_inputs(seed: int = 0) -> dict:
    key = jax.random.key(seed)
    ks = jax.random.split(key, 16)
    f32 = jnp.float32
    L = DEPTH

    def nrm(k, shape, fan_in, mult=1.0):
        return jax.random.normal(k, shape, f32) * (mult * fan_in ** -0.5)

    kb1, kb2 = jax.random.split(ks[7])
    i_bias = 0.1 * jax.random.normal(kb1, (L, 4, M_HEADS), f32)
    f_bias = 3.0 + 3.0 * jax.random.uniform(kb2, (L, 4, M_HEADS), f32)
    is_forget = jnp.array([0.0, 1.0, 0.0, 1.0], f32)[None, :, None]
    gate_bias = (is_forget * f_bias + (1.0 - is_forget) * i_bias).reshape(L, 4 * M_HEADS)

    return {
        "x": jax.random.normal(ks[0], (BATCH, SEQ, D_MODEL), f32),
        "c": jax.random.normal(ks[1], (BATCH, D_MODEL), f32),
        "w_ada": nrm(ks[2], (L, D_MODEL, 6 * D_MODEL), D_MODEL),
        "b_ada": 0.02 * jax.random.normal(ks[3], (L, 6 * D_MODEL), f32),
        "norm1_w": 1.0 + 0.05 * jax.random.normal(ks[4], (L, D_MODEL), f32),
        "w_in_mix": nrm(ks[5], (L, D_MODEL, IN_COLS), D_MODEL),
        "conv_w": nrm(ks[6], (L, CONV_K, CONV_WIDTH), CONV_K),
        "mlstm_gate_bias": gate_bias,
        "mlstm_norm_w": 1.0 + 0.05 * jax.random.normal(ks[8], (L, M_V), f32),
        "w_conv_out": nrm(ks[9], (L, CONV_WIDTH, D_MODEL), CONV_WIDTH),
        "w_mlstm_out": nrm(ks[10], (L, M_V, D_MODEL), M_V),
        "w_o": nrm(ks[11], (L, D_MODEL, D_MODEL), D_MODEL),
        "norm2_w": 1.0 + 0.05 * jax.random.normal(ks[12], (L, D_MODEL), f32),
        "w_gate_up": nrm(ks[13], (L, D_MODEL, 2 * FFN_HIDDEN), D_MODEL),
        "w_down": nrm(ks[14], (L, FFN_HIDDEN, D_MODEL), FFN_HIDDEN),
        "final_norm_w": 1.0 + 0.05 * jax.random.normal(ks[15], (D_MODEL,), f32),
    }


def reference(x, c, w_ada, b_ada, norm1_w, w_in_mix, conv_w, mlstm_gate_bias, mlstm_norm_w,
              w_conv_out, w_mlstm_out, w_o, norm2_w, w_gate_up, w_down, final_norm_w):
    c_act = jax.nn.silu(c)
    for layer in range(DEPTH):
        ada = jnp.einsum('bd,de->be', c_act, w_ada[layer]) + b_ada[layer]
        shift1, scale1, gate1, shift2, scale2, gate2 = jnp.split(ada, 6, axis=-1)
        h = modulate(rmsnorm(x, norm1_w[layer]), shift1, scale1)
        mix = mixer(h, w_in_mix[layer], conv_w[layer], mlstm_gate_bias[layer], mlstm_norm_w[layer],
                    w_conv_out[layer], w_mlstm_out[layer], w_o[layer])
        x = x + gate1[:, None, :] * mix
        h = modulate(rmsnorm(x, norm2_w[layer]), shift2, scale2)
        x = x + gate2[:, None, :] * swiglu(h, w_gate_up[layer], w_down[layer])
    return rmsnorm(x, final_norm_w)
```

```python
import numpy as np
import concourse.bass as bass
import concourse.mybir as mybir
from concourse.bass_utils import run_bass_kernel_spmd

F32 = mybir.dt.float32
BF16 = mybir.dt.bfloat16
AF = mybir.ActivationFunctionType
ALU = mybir.AluOpType

D = 2048
S = 4096
NCORES = 8
T = 512
NT = S // T
KC = D // 128
EPS = 1e-6
H = 8
DK = 128
DV = 256
L = 128
NCH = S // L
FF = 5632
FKC = FF // 128
IN_COLS = 13344
OFF_CB, OFF_CC, OFF_CX, OFF_Q, OFF_K, OFF_V, OFF_O, OFF_G, OFF_BG = 0, 1024, 2048, 3072, 4096, 5120, 7168, 9216, 9248


class Op:
    __slots__ = ("eng", "fn", "deps", "is_dma", "sig", "count", "dsem", "dval", "idx", "xprev")


class Tracker:
    ENGS = ("pe", "act", "dve", "pool", "sp")

    def __init__(self, nc, n_dma_sems):
        self.nc = nc
        self.ops = {e: [] for e in self.ENGS}
        self.last_writer = {}
        self.readers = {}
        self.esem = {e: nc.alloc_semaphore("c_" + e) for e in ("pe", "act", "dve", "pool")}
        self.dsems = {q: [nc.alloc_semaphore("d_%s%d" % (q, i)) for i in range(n)] for q, n in n_dma_sems.items()}
        self.drr = {q: 0 for q in n_dma_sems}
        self.duse = {}
        self.dlast = {}

    def op(self, eng, fn, reads=(), writes=(), dma=False, excl=()):
        o = Op()
        o.eng, o.fn, o.is_dma, o.sig, o.count = eng, fn, dma, False, 0
        o.idx = len(self.ops[eng])
        deps = {}
        for k in excl:
            w = self.last_writer.get(k)
            if w is not None and w.eng != eng:
                deps[id(w)] = w
            elif w is not None and w.eng == eng:
                pw = getattr(w, "xprev", {}).get(k)
                if pw is not None and pw.eng != eng:
                    deps[id(pw)] = pw
        for k in reads:
            w = self.last_writer.get(k)
            if w is not None:
                deps[id(w)] = w
        for k in writes:
            w = self.last_writer.get(k)
            if w is not None:
                deps[id(w)] = w
            for r in self.readers.get(k, {}).values():
                for rr in r:
                    deps[id(rr)] = rr
        dl = []
        for d in deps.values():
            if d is o:
                continue
            if (not d.is_dma) and d.eng == eng and not dma:
                if eng == "pe":
                    continue
            dl.append(d)
        o.deps = dl
        if dma:
            sems = self.dsems[eng]
            s = sems[self.drr[eng] % len(sems)]
            self.drr[eng] += 1
            self.duse[s] = self.duse.get(s, 0) + 1
            o.dsem, o.dval = s, 16 * self.duse[s]
            prev = self.dlast.get(s)
            if prev is not None:
                o.deps.append(prev)
            self.dlast[s] = o
        if excl:
            o.xprev = {}
            for k in excl:
                w = self.last_writer.get(k)
                if w is not None:
                    o.xprev[k] = w if w.eng != eng else getattr(w, "xprev", {}).get(k)
                self.last_writer[k] = o
                self.readers[k] = {}
        for k in writes:
            self.last_writer[k] = o
            self.readers[k] = {}
        for k in reads:
            rd = self.readers.setdefault(k, {})
            if dma:
                rd.setdefault("dma", []).append(o)
            else:
                rd[eng] = [o]
        self.ops[eng].append(o)
        return o

    def barrier(self):
        lasts = []
        for e in self.ENGS:
            for o in reversed(self.ops[e]):
                if not o.is_dma and o.fn is not None:
                    lasts.append(o)
                    break
        dl = [x for x in self.dlast.values() if x.eng != "pool"]
        for e in self.ENGS:
            o = Op()
            o.eng, o.fn, o.is_dma, o.sig, o.count = e, None, False, False, 0
            o.idx = len(self.ops[e])
            o.deps = [x for x in lasts if x.eng != e] + dl
            self.ops[e].append(o)
        self.last_writer = {k: v for k, v in self.last_writer.items() if k[0] == "wscr"}
        self.readers = {}

    def emit(self, block):
        for e in self.ENGS:
            for o in self.ops[e]:
                for d in o.deps:
                    if not d.is_dma:
                        d.sig = True
        for e in self.ENGS:
            c = 0
            for o in self.ops[e]:
                if o.sig:
                    c += 1
                    o.count = c
        fin = Op()
        fin.eng, fin.fn, fin.is_dma, fin.sig, fin.count = "sp", None, False, False, 0
        fin.deps = list(self.dlast.values())
        self.ops["sp"].append(fin)

        def run(e, h):
            waited = {}
            for o in self.ops[e]:
                for d in o.deps:
                    if d.is_dma:
                        sem, val = d.dsem, d.dval
                    else:
                        sem, val = self.esem[d.eng], d.count
                    if waited.get(sem.num, 0) >= val:
                        continue
                    waited[sem.num] = val
                    h.wait_ge(sem, val)
                if o.fn is None:
                    continue
                ins = o.fn(h)
                if o.is_dma:
                    ins.then_inc(o.dsem, 16)
                elif o.sig:
                    ins.then_inc(self.esem[e], 1)

        @block.tensor
        def _(h):
            run("pe", h)

        @block.scalar
        def _(h):
            run("act", h)

        @block.vector
        def _(h):
            run("dve", h)

        @block.gpsimd
        def _(h):
            run("pool", h)

        @block.sync
        def _(h):
            run("sp", h)


def build_program(dbg=False, phases="0ABC"):
    nc = bass.Bass("TRN2", target_bir_lowering=False)
    tr = Tracker(nc, {"sp": 24, "pool": 12})

    def din(name, shape, dt=F32):
        return nc.dram_tensor(name, list(shape), dt, kind="ExternalInput").ap()

    def dscr(name, shape, dt):
        return nc.dram_tensor(name, list(shape), dt, kind=("ExternalOutput" if dbg else "Internal")).ap()

    xT = din("xT", [D, S])
    ccol = din("ccol", [128, KC])
    w_ada = din("w_ada", [D, 6 * D])
    bada = din("bada", [128, 96])
    n1w = din("n1w", [128, KC])
    n2w = din("n2w", [128, KC])
    fnw = din("fnw", [128, KC])
    w_in = din("w_in", [D, IN_COLS])
    convw = din("convw", [128, 24])
    gbias = din("gbias", [128, 32])
    mnw = din("mnw", [128, D])
    w_co = din("w_co", [1024, D])
    w_mo = din("w_mo", [D, D])
    w_oo = din("w_oo", [D, D])
    w_gu = din("w_gu", [D, 2 * FF])
    w_dn = din("w_dn", [FF, D])
    cmat = din("cmat", [128, 4, 128])
    outT = nc.dram_tensor("outT", [D, S], F32, kind="ExternalOutput").ap()

    qT_s = dscr("qT_s", [H * DK, S], BF16)
    kT_s = dscr("kT_s", [H * DK, S], BF16)
    k_s = dscr("k_s", [S, H * DK], BF16)
    v_s = dscr("v_s", [S, H * DV], BF16)
    g_s = dscr("g_s", [S, 32], F32)
    p_s = dscr("p_s", [1024, S + 2], F32)
    hm_s = dscr("hm_s", [H * DV, S], BF16)
    ada_dbg = dscr("ada_dbg", [128, 96], F32) if dbg else None

    class WB:
        pass
    wblocks = []

    def mk_block(name, src, k0, kn, c0, width):
        b = WB()
        b.name, b.src, b.k0, b.kn, b.c0, b.width = name, src, k0, kn, c0, width
        b.scr = nc.dram_tensor("wb_" + name, [128, kn * width], BF16, kind="Internal").ap()
        b.converted = False
        wblocks.append(b)
        return b

    def convert(b):
        if b.converted:
            return
        b.converted = True
        srcv = b.src.rearrange("(kc p) e -> p kc e", p=128)
        dstv = b.scr.rearrange("p (kc e) -> p kc e", e=b.width)
        step = 4
        for i, k in enumerate(range(0, b.kn, step)):
            kk = min(step, b.kn - k)
            tr.op("pool", lambda h, k=k, kk=kk: h.dma_start(
                out=dstv[:, k:k + kk, :], in_=srcv[:, b.k0 + k:b.k0 + k + kk, b.c0:b.c0 + b.width]),
                reads=(), writes=[("wscr", b.name, i)], dma=True)
        b.npieces = (b.kn + step - 1) // step

    WIN = {}
    for nm, off, n in (("cb", OFF_CB, 2), ("cc", OFF_CC, 2), ("cx", OFF_CX, 2), ("q", OFF_Q, 2), ("k", OFF_K, 2),
                       ("v", OFF_V, 4), ("o", OFF_O, 4), ("bgc", OFF_BG, 4), ("bgm", OFF_BG + D, 4)):
        WIN[nm] = [mk_block("%s%d" % (nm, j), w_in, 0, KC, off + 512 * j, 512) for j in range(n)]
    WG = mk_block("g", w_in, 0, KC, OFF_G, 32)
    WCO = [mk_block("co%d" % j, w_co, 0, 8, 512 * j, 512) for j in range(4)]
    WMO = [mk_block("mo%d" % j, w_mo, 0, KC, 512 * j, 512) for j in range(4)]
    WOO = [mk_block("oo%d" % j, w_oo, 0, KC, 512 * j, 512) for j in range(4)]
    WGG = [mk_block("gg%d" % j, w_gu, 0, KC, 512 * j, 512) for j in range(11)]
    WGU = [mk_block("gu%d" % j, w_gu, 0, KC, FF + 512 * j, 512) for j in range(11)]
    WDN = [[mk_block("dn%d_%d" % (j, q), w_dn, 11 * q, 11, 512 * j, 512) for q in range(4)] for j in range(4)]

    class Alloc:
        def __init__(self, base):
            self.off = base

        def get(self, name, shape, dt):
            nbytes = int(np.prod(shape[1:])) * (2 if dt == BF16 else 4)
            nbytes = (nbytes + 63) // 64 * 64
            t = nc.alloc_sbuf_tensor_at(name, list(shape), dt, offset=self.off)
            self.off += nbytes
            assert self.off <= 229120, (name, self.off)
            return t

    al = Alloc(16640)
    c_mat = al.get("c_mat", [128, 4, 128], F32)
    c_bf = al.get("c_bf", [128, 4, 128], BF16)
    sm_f = al.get("sm_f", [128, 8 * KC + 96 + 24 + 32 + 16], F32)
    o_ = 0
    def smv(n):
        nonlocal o_
        v = sm_f[:, o_:o_ + n]
        o_ += n
        return v
    s_ccol = smv(KC); s_n1w = smv(KC); s_n2w = smv(KC); s_fnw = smv(KC)
    s_a1 = smv(KC); s_a2 = smv(KC); s_tmp16 = smv(KC); s_spare = smv(KC)
    s_ada = smv(96); s_convw = smv(24); s_gbias = smv(32); s_bada_unused = smv(16)
    s_bada = al.get("s_bada", [128, 96], F32)
    s_epst = al.get("s_epst", [128, 8], F32)
    s_eps = s_epst[:, 0:1]
    cact = al.get("cact", [128, KC], BF16)
    base_persist = al.off

    ps = nc.alloc_psum_tensor("ps", [128, 8, 512], F32)
    triF, triB, onesm, ident = (c_bf[:, i, :] for i in range(4))

    rot = [0]

    def bank():
        b = rot[0] % 6
        rot[0] += 1
        return b

    def sp_load(dst, src, wkeys, rkeys=()):
        return tr.op("sp", lambda h: h.dma_start(out=dst, in_=src), reads=list(rkeys), writes=list(wkeys), dma=True)

    sp_load(c_mat[:], cmat, [("c_mat",)])
    sp_load(sm_f[:, 0:KC], ccol, [("ccol",)])
    sp_load(s_n1w, n1w, [("n1w",)])
    sp_load(s_n2w, n2w, [("n2w",)])
    sp_load(s_fnw, fnw, [("fnw",)])
    sp_load(s_convw, convw, [("convw",)])
    sp_load(s_gbias, gbias, [("gbias",)])
    sp_load(s_bada[:], bada, [("bada",)])
    tr.op("dve", lambda h: h.tensor_copy(out=c_bf[:], in_=c_mat[:]), reads=[("c_mat",)], writes=[("c_bf",)])
    tr.op("dve", lambda h: h.memset(s_epst[:], EPS), writes=[("eps",)])
    tr.op("act", lambda h: h.activation(out=cact[:], in_=s_ccol, func=AF.Silu), reads=[("ccol",)], writes=[("cact",)])

    al0 = Alloc(base_persist)
    wada = [al0.get("wada%d" % i, [128, KC, 512], BF16) for i in range(2)]
    w_ada_v = w_ada.rearrange("(kc p) e -> p kc e", p=128)
    ps_ada = ps[:, 6, 0:96]
    NB0 = 8
    for blk in range(NB0):
        wt = wada[blk % 2]
        for qd in range(4):
            tr.op("pool", lambda h, wt=wt, qd=qd, blk=blk: h.dma_start(
                out=wt[:, 4 * qd:4 * qd + 4, :], in_=w_ada_v[:, 4 * qd:4 * qd + 4, 512 * blk:512 * blk + 512]),
                writes=[("wada", blk % 2, qd)], dma=True)
        for sub in range(4):
            col = blk * 4 + sub
            for kc in range(KC):
                tr.op("pe", lambda h, wt=wt, sub=sub, kc=kc, col=col: h.matmul(
                    ps_ada[:, col:col + 1], lhsT=wt[:, kc, sub * 128:(sub + 1) * 128], rhs=cact[:, kc:kc + 1],
                    start=(kc == 0), stop=(kc == KC - 1)),
                    reads=[("wada", blk % 2, kc // 4), ("cact",)], writes=[("psada",)])
    tr.op("dve", lambda h: h.tensor_tensor(out=s_ada[:, 0:32], in0=ps_ada[:, 0:32], in1=s_bada[:, 0:32], op=ALU.add),
          reads=[("psada",), ("bada",)], writes=[("ada",)])
    b1 = s_ada[:, 0:16]; g1 = s_ada[:, 32:48]; b2 = s_ada[:, 48:64]; g2 = s_ada[:, 80:96]
    tr.op("dve", lambda h: h.scalar_tensor_tensor(out=s_a1, in0=s_ada[:, 16:32], scalar=1.0, in1=s_n1w,
                                                   op0=ALU.add, op1=ALU.mult),
          reads=[("ada",), ("n1w",)], writes=[("a1",)])

    PA_BLOCKS = WIN["cc"] + WIN["cx"] + WIN["q"] + WIN["k"] + WIN["v"] + [WG]
    for b in PA_BLOCKS:
        convert(b)

    zt = al0.get("zt", [128, 8], F32)
    tr.op("dve", lambda h: h.memset(zt[:], 0.0), writes=[("zt",)])
    p_v = p_s.rearrange("(c p) t -> p c t", p=128)
    tr.op("sp", lambda h: h.dma_start(out=p_v[:, :, 0:1], in_=zt[:, 0:8].rearrange("p (c o) -> p c o", o=1), allow_slow_non_contiguous=True),
          reads=[("zt",)], writes=[("p_pad0",)], dma=True)
    tr.op("sp", lambda h: h.dma_start(out=p_v[:, :, S + 1:S + 2], in_=zt[:, 0:8].rearrange("p (c o) -> p c o", o=1), allow_slow_non_contiguous=True),
          reads=[("zt",)], writes=[("p_pad1",)], dma=True)

    tr.barrier()

    class WStream:
        def __init__(self, alloc, nslots=2):
            self.slots = [alloc.get("wsl%d" % i, [128, KC * 512], BF16) for i in range(nslots)]
            self.n = 0

        def load(self, b):
            i = self.n % len(self.slots)
            self.n += 1
            sl = self.slots[i]
            dst = sl[:, 0:b.kn * b.width]
            tr.op("sp", lambda h: h.dma_start(out=dst, in_=b.scr),
                  reads=[("wscr", b.name, j) for j in range(b.npieces)], writes=[("wsl", i)], dma=True)
            return (sl[:, 0:b.kn * b.width].rearrange("p (kc e) -> p kc e", e=b.width), ("wsl", i))

    def run_blocks(ws, tasks):
        pend = [ws.load(tasks[0][0])] if tasks else []
        for i, (b, fn) in enumerate(tasks):
            if i + 1 < len(tasks):
                pend.append(ws.load(tasks[i + 1][0]))
            wv, wk = pend.pop(0)
            fn(wv, wk)

    def norm_stats(xs, sq, rstd, xkey, sqkey=lambda kc: ("sq", kc)):
        for kc in range(KC):
            tr.op("act", lambda h, kc=kc: h.activation(out=sq[:, kc, :], in_=xs[:, kc, :], func=AF.Square),
                  reads=[(xkey, kc)], writes=[sqkey(kc)])
        for kc in range(KC):
            tr.op("pe", lambda h, kc=kc: h.matmul(ps[:, 6, :], lhsT=onesm, rhs=sq[:, kc, :],
                                                  start=(kc == 0), stop=(kc == KC - 1)),
                  reads=[sqkey(kc), ("c_bf",)], writes=[("ps", 6)])
        tr.op("act", lambda h: h.activation(out=rstd[:], in_=ps[:, 6, :], func=AF.Sqrt, bias=s_eps, scale=1.0),
              reads=[("ps", 6), ("eps",)], writes=[("rstd0",)])
        tr.op("dve", lambda h: h.reciprocal(out=rstd[:], in_=rstd[:]),
              reads=[("rstd0",)], writes=[("rstd",)])

    def norm_apply(xs, xkey, rstd, a, akey, b, bkey, hT, tmp, inplace):
        for kc in range(KC):
            if inplace:
                dst, dkey = xs[:, kc, :], (xkey, kc)
            else:
                dst, dkey = tmp[:, kc % 2, :], ("ntmp", kc % 2)
            tr.op("dve", lambda h, kc=kc, dst=dst: h.scalar_tensor_tensor(
                out=dst, in0=xs[:, kc, :], scalar=a[:, kc:kc + 1], in1=rstd[:], op0=ALU.mult, op1=ALU.mult),
                reads=[(xkey, kc), ("rstd",), akey], writes=[dkey])
            tr.op("act", lambda h, kc=kc, dst=dst: h.activation(
                out=hT[:, kc, :], in_=dst, func=AF.Identity, bias=b[:, kc:kc + 1], scale=1.0),
                reads=[dkey, bkey], writes=[("hT", kc)])

    def fm_tiles(wv, wk, act_t, akey, kn, fn_evac, ntiles=4):
        for c in range(ntiles):
            bk = bank()
            for kc in range(kn):
                tr.op("pe", lambda h, c=c, kc=kc, bk=bk: h.matmul(
                    ps[:, bk, :], lhsT=wv[:, kc, c * 128:(c + 1) * 128], rhs=act_t[:, kc, :],
                    start=(kc == 0), stop=(kc == kn - 1)),
                    reads=[wk, (akey(kc) if callable(akey) else (akey, kc))], writes=[("ps", bk)])
            fn_evac(c, ps[:, bk, :], ("ps", bk))

    if "A" in phases:
        alA = Alloc(base_persist)
        xs = alA.get("xsA", [128, KC, T], F32)
        hT = alA.get("hTA", [128, KC, T], BF16)
        sq = alA.get("sqA", [128, KC, T], BF16)
        rstd = alA.get("rstdA", [128, T], F32)
        ws = WStream(alA)
        st_q = alA.get("st_q", [128, 8, T], BF16)
        st_k = alA.get("st_k", [128, 8, T], BF16)
        st_kt = alA.get("st_kt", [128, 4, 1024], BF16)
        st_vt = alA.get("st_vt", [128, 4, 2048], BF16)
        st_g = alA.get("st_g", [128, 4, 32], F32)
        st_cc = alA.get("st_cc", [128, 8, T], F32)
        st_p = alA.get("st_p", [128, 8, T], F32)
        xT_v = xT.rearrange("(kc p) t -> p kc t", p=128)
        qT_v = qT_s.rearrange("(h p) t -> p h t", p=128)
        kT_v = kT_s.rearrange("(h p) t -> p h t", p=128)
        KS = float(DK ** -0.5)

        def tileA(it):
            t0 = it * T
            for half in range(2):
                tr.op("sp", lambda h, half=half, t0=t0: h.dma_start(
                    out=xs[:, 8 * half:8 * half + 8, :], in_=xT_v[:, 8 * half:8 * half + 8, t0:t0 + T]),
                    writes=[("xs", kc) for kc in range(8 * half, 8 * half + 8)], dma=True)
            norm_stats(xs, sq, rstd, "xs")
            norm_apply(xs, "xs", rstd, s_a1, ("a1",), b1, ("ada",), hT, None, True)
            tasks = []

            def t_cc(j):
                def f(wv, wk):
                    def ev(c, pt, pk):
                        tr.op("act", lambda h: h.activation(out=st_cc[:, 4 * j + c, :], in_=pt, func=AF.Copy),
                              reads=[pk], writes=[("st_cc", 4 * j + c)])
                    fm_tiles(wv, wk, hT, "hT", KC, ev)
                return f

            def t_cx(j):
                def f(wv, wk):
                    def ev(c, pt, pk):
                        tr.op("dve", lambda h: h.tensor_tensor(out=st_p[:, 4 * j + c, :], in0=pt,
                                                               in1=st_cc[:, 4 * j + c, :], op=ALU.mult),
                              reads=[pk, ("st_cc", 4 * j + c)], writes=[("st_p", 4 * j + c)])
                    fm_tiles(wv, wk, hT, "hT", KC, ev)
                    if j == 1:
                        tr.op("sp", lambda h: h.dma_start(out=p_v[:, :, 1 + t0:1 + t0 + T], in_=st_p[:]),
                              reads=[("st_p", i) for i in range(8)], writes=[("p_s", it)], dma=True)
                return f

            def t_q(j):
                def f(wv, wk):
                    def ev(c, pt, pk):
                        tr.op("act", lambda h: h.activation(out=st_q[:, 4 * j + c, :], in_=pt, func=AF.Copy),
                              reads=[pk], writes=[("st_q", 4 * j + c)])
                    fm_tiles(wv, wk, hT, "hT", KC, ev)
                    if j == 1:
                        tr.op("sp", lambda h: h.dma_start(out=qT_v[:, :, t0:t0 + T], in_=st_q[:]),
                              reads=[("st_q", i) for i in range(8)], writes=[("qT_s", it)], dma=True)
                return f

            def t_k(j):
                def f(wv, wk):
                    def ev(c, pt, pk):
                        tr.op("dve", lambda h: h.tensor_scalar(out=st_k[:, 4 * j + c, :], in0=pt, scalar1=KS,
                                                               scalar2=None, op0=ALU.mult),
                              reads=[pk], writes=[("st_k", 4 * j + c)])
                    fm_tiles(wv, wk, hT, "hT", KC, ev)
                    for tsub in range(4):
                        bk = bank()
                        for kc in range(KC):
                            tr.op("pe", lambda h, kc=kc, bk=bk, tsub=tsub: h.matmul(
                                ps[:, bk, :], lhsT=hT[:, kc, tsub * 128:(tsub + 1) * 128], rhs=wv[:, kc, :],
                                start=(kc == 0), stop=(kc == KC - 1)),
                                reads=[wk, ("hT", kc)], writes=[("ps", bk)])
                        tr.op("act", lambda h, bk=bk, tsub=tsub: h.activation(
                            out=st_kt[:, tsub, 512 * j:512 * j + 512], in_=ps[:, bk, :], func=AF.Identity, scale=KS),
                            reads=[("ps", bk)], writes=[("st_kt", tsub, j)])
                    if j == 1:
                        tr.op("sp", lambda h: h.dma_start(out=kT_v[:, :, t0:t0 + T], in_=st_k[:]),
                              reads=[("st_k", i) for i in range(8)], writes=[("kT_s", it)], dma=True)
                        tr.op("sp", lambda h: h.dma_start(
                            out=k_s[t0:t0 + T, :].rearrange("(s p) e -> p s e", p=128), in_=st_kt[:]),
                            reads=[("st_kt", a, b_) for a in range(4) for b_ in range(2)], writes=[("k_s", it)], dma=True)
                return f

            def t_v(j):
                def f(wv, wk):
                    for tsub in range(4):
                        bk = bank()
                        for kc in range(KC):
                            tr.op("pe", lambda h, kc=kc, bk=bk, tsub=tsub: h.matmul(
                                ps[:, bk, :], lhsT=hT[:, kc, tsub * 128:(tsub + 1) * 128], rhs=wv[:, kc, :],
                                start=(kc == 0), stop=(kc == KC - 1)),
                                reads=[wk, ("hT", kc)], writes=[("ps", bk)])
                        eng = "act" if tsub % 2 == 0 else "dve"
                        if eng == "act":
                            tr.op("act", lambda h, bk=bk, tsub=tsub: h.activation(
                                out=st_vt[:, tsub, 512 * j:512 * j + 512], in_=ps[:, bk, :], func=AF.Copy),
                                reads=[("ps", bk)], writes=[("st_vt", tsub, j)])
                        else:
                            tr.op("dve", lambda h, bk=bk, tsub=tsub: h.tensor_copy(
                                out=st_vt[:, tsub, 512 * j:512 * j + 512], in_=ps[:, bk, :]),
                                reads=[("ps", bk)], writes=[("st_vt", tsub, j)])
                    if j == 3:
                        tr.op("sp", lambda h: h.dma_start(
                            out=v_s[t0:t0 + T, :].rearrange("(s p) e -> p s e", p=128), in_=st_vt[:]),
                            reads=[("st_vt", a, b_) for a in range(4) for b_ in range(4)], writes=[("v_s", it)], dma=True)
                return f

            def t_g(wv, wk):
                for tsub in range(4):
                    bk = bank()
                    for kc in range(KC):
                        tr.op("pe", lambda h, kc=kc, bk=bk, tsub=tsub: h.matmul(
                            ps[:, bk, 0:32], lhsT=hT[:, kc, tsub * 128:(tsub + 1) * 128], rhs=wv[:, kc, :],
                            start=(kc == 0), stop=(kc == KC - 1)),
                            reads=[wk, ("hT", kc)], writes=[("ps", bk)])
                    tr.op("dve", lambda h, bk=bk, tsub=tsub: h.tensor_tensor(
                        out=st_g[:, tsub, :], in0=ps[:, bk, 0:32], in1=s_gbias, op=ALU.add),
                        reads=[("ps", bk), ("gbias",)], writes=[("st_g", tsub)])
                tr.op("sp", lambda h: h.dma_start(
                    out=g_s[t0:t0 + T, :].rearrange("(s p) e -> p s e", p=128), in_=st_g[:]),
                    reads=[("st_g", a) for a in range(4)], writes=[("g_s", it)], dma=True)

            for j in range(2):
                tasks.append((WIN["cc"][j], t_cc(j)))
            for j in range(2):
                tasks.append((WIN["cx"][j], t_cx(j)))
            for j in range(2):
                tasks.append((WIN["q"][j], t_q(j)))
            for j in range(2):
                tasks.append((WIN["k"][j], t_k(j)))
            for j in range(4):
                tasks.append((WIN["v"][j], t_v(j)))
            tasks.append((WG, t_g))
            run_blocks(ws, tasks)

        for it in range(NT):
            tileA(it)
        tr.barrier()

    if "B" in phases:
        NHP = 2
        VW = 264
        NG = NCH * 8
        alB = Alloc(base_persist)
        eb = alB.get("eb", [128, 2, NG], F32)
        er = alB.get("er", [128, 2, NG], F32)
        ed = alB.get("ed", [128, 2, NG], F32)
        ebinv = alB.get("ebinv", [128, 2, NG], F32)
        mnw_sb = alB.get("mnw_sb", [128, D], F32)
        qTh = [alB.get("qTh%d" % i, [128, S], BF16) for i in range(2)]
        kTh = [alB.get("kTh%d" % i, [128, S], BF16) for i in range(2)]
        kth = [alB.get("kth%d" % i, [128, NCH, 128], BF16) for i in range(2)]
        vh = [alB.get("vh%d" % i, [128, NCH, VW], BF16) for i in range(2)]
        hbuf = alB.get("hbuf", [128, 2, NCH, DV], BF16)
        hmT_st = alB.get("hmT_st", [128, 2, 2, S], BF16)
        base_small = alB.off
        alP = Alloc(base_small)
        gsb = alP.get("gsb", [128, NCH, 32], F32)
        nf = alP.get("nf", [128, 2, NG], F32)
        nr = alP.get("nr", [128, 2, NG], F32)
        nfh = alP.get("nfh", [128, 3, 2 * NG], BF16)
        one_t = alP.get("one_t", [128, 8], F32)
        alS = Alloc(base_small)
        Cf = alS.get("Cf", [128, 4, VW], F32)
        Cb = alS.get("Cb", [128, 4, VW], BF16)
        smt = alS.get("smt", [128, 4, 128], BF16)
        vp = alS.get("vp", [128, 4, VW], BF16)
        dn = alS.get("dn", [128, 4, 8], F32)
        ssq = alS.get("ssq", [128, 4, 8], F32)
        hs = alS.get("hs", [128, 4, DV], F32)
        junk = alS.get("junk", [128, 4, DV], BF16)
        hmb = alS.get("hmb", [128, 4, DV], F32)

        g_v = g_s.rearrange("(c p) g -> p c g", p=128)
        for qd in range(4):
            tr.op("sp", lambda h, qd=qd: h.dma_start(out=gsb[:, 8 * qd:8 * qd + 8, :], in_=g_v[:, 8 * qd:8 * qd + 8, :]),
                  writes=[("gsb", qd)], dma=True)
        sp_load(mnw_sb[:], mnw, [("mnw",)])
        tr.op("dve", lambda h: h.memset(one_t[:], 1.0), writes=[("one",)])
        for sl in range(2):
            tr.op("dve", lambda h, sl=sl: h.memset(vh[sl][:, :, DV:DV + 1], 1.0), writes=[("vone", sl)])
        gkeys = [("gsb", qd) for qd in range(4)]
        for d in range(2):
            fcol = 8 + 16 * d
            nfd = nf[:, d, :].rearrange("p (c h) -> p c h", h=8)
            tr.op("act", lambda h, d=d, fcol=fcol, nfd=nfd: h.activation(
                out=nfd, in_=gsb[:, :, fcol:fcol + 8], func=AF.Exp, scale=-1.0),
                reads=gkeys, writes=[("nf", d)])
            tr.op("act", lambda h, d=d: h.activation(
                out=nf[:, d, :], in_=nf[:, d, :], func=AF.Ln, bias=one_t[:, 0:1], scale=1.0),
                reads=[("nf", d), ("one",)], writes=[("nf", d)])
        nf2 = nf[:].rearrange("p d n -> p (d n)")
        nr2 = nr[:].rearrange("p d n -> p (d n)")
        tr.op("dve", lambda h: h.tensor_copy(out=nfh[:, 0, :], in_=nf2), reads=[("nf", 0), ("nf", 1)], writes=[("nfh", 0)])
        tr.op("dve", lambda h: h.tensor_tensor(out=nr2, in0=nf2, in1=nfh[:, 0, :], op=ALU.subtract),
              reads=[("nf", 0), ("nf", 1), ("nfh", 0)], writes=[("nr",)])
        tr.op("dve", lambda h: h.tensor_copy(out=nfh[:, 1, :], in_=nr2), reads=[("nr",)], writes=[("nfh", 1)])
        tr.op("dve", lambda h: h.tensor_tensor(out=nr2, in0=nr2, in1=nfh[:, 1, :], op=ALU.subtract),
              reads=[("nr",), ("nfh", 1)], writes=[("nr",)])
        tr.op("dve", lambda h: h.tensor_copy(out=nfh[:, 2, :], in_=nr2), reads=[("nr",)], writes=[("nfh", 2)])
        for d in range(2):
            tri = triF if d == 0 else triB
            for part in range(3):
                tr.op("pe", lambda h, d=d, part=part, tri=tri: h.matmul(
                    ps[:, d, 0:NG], lhsT=tri, rhs=nfh[:, part, d * NG:(d + 1) * NG], start=(part == 0), stop=(part == 2)),
                    reads=[("nfh", part), ("c_bf",)], excl=[("ps", d)])
            for part in range(3):
                tr.op("pe", lambda h, d=d, part=part: h.matmul(
                    ps[:, 2 + d, 0:NG], lhsT=onesm, rhs=nfh[:, part, d * NG:(d + 1) * NG], start=(part == 0), stop=(part == 2)),
                    reads=[("nfh", part), ("c_bf",)], excl=[("ps", 2 + d)])
            icol = 16 * d
            tr.op("act", lambda h, d=d: h.activation(out=eb[:, d, :], in_=ps[:, d, 0:NG], func=AF.Exp, scale=-1.0),
                  excl=[("ps", d)], writes=[("eb", d)])
            tr.op("act", lambda h, d=d: h.activation(out=ebinv[:, d, :], in_=ps[:, d, 0:NG], func=AF.Exp, scale=1.0),
                  excl=[("ps", d)], writes=[("ebinv", d)])
            tr.op("act", lambda h, d=d: h.activation(out=ed[:, d, :], in_=ps[:, 2 + d, 0:NG], func=AF.Exp, scale=-float(D)),
                  excl=[("ps", 2 + d)], writes=[("ed", d)])
            igc = nr[:, d, :]
            tr.op("act", lambda h, d=d, icol=icol, igc=igc: h.activation(
                out=igc.rearrange("p (c h) -> p c h", h=8), in_=gsb[:, :, icol:icol + 8], func=AF.Exp, scale=1.0),
                reads=gkeys + [("nr",)], writes=[("igc", d)])
            tr.op("dve", lambda h, d=d, igc=igc: h.tensor_tensor(out=er[:, d, :], in0=ebinv[:, d, :], in1=igc, op=ALU.mult),
                  reads=[("ebinv", d), ("igc", d)], writes=[("er", d)])
        tr.barrier()

        wada2 = [alS.get("wada2_%d" % i, [128, KC, 256], BF16) for i in range(2)]
        ps_ada2 = ps[:, 4, 400:464]
        NHB = 32
        ada_state = [0]

        def ada_load(j):
            wt = wada2[j % 2]
            c0 = 4096 + 256 * j
            for qd in range(4):
                tr.op("pool", lambda h, qd=qd: h.dma_start(
                    out=wt[:, 4 * qd:4 * qd + 4, :], in_=w_ada_v[:, 4 * qd:4 * qd + 4, c0:c0 + 256]),
                    writes=[("wada2", j % 2, qd)], dma=True)

        def ada_mm(j):
            wt = wada2[j % 2]
            for sub in range(2):
                col = 2 * j + sub
                for kc in range(KC):
                    tr.op("pe", lambda h, sub=sub, kc=kc, col=col: h.matmul(
                        ps_ada2[:, col:col + 1], lhsT=wt[:, kc, sub * 128:(sub + 1) * 128], rhs=cact[:, kc:kc + 1],
                        start=(kc == 0), stop=(kc == KC - 1)),
                        reads=[("wada2", j % 2, kc // 4), ("cact",)], excl=[("ps", 4)])

        conv_q = [b for b in wblocks if not b.converted]

        def conv_some(n):
            for _ in range(n):
                if conv_q:
                    convert(conv_q.pop(0))

        ada_load(0)
        ada_load(1)
        conv_some(4)

        k_v = k_s.rearrange("(c p) e -> p c e", p=128)
        v_v = v_s.rearrange("(c p) e -> p c e", p=128)
        hm_v = hm_s.rearrange("(v p) t -> p v t", p=128)

        def load_head(hd):
            sl = hd % 2
            tr.op("sp", lambda h: h.dma_start(out=qTh[sl][:], in_=qT_s[hd * 128:(hd + 1) * 128, :]),
                  writes=[("qTh", sl)], dma=True)
            tr.op("sp", lambda h: h.dma_start(out=kTh[sl][:], in_=kT_s[hd * 128:(hd + 1) * 128, :]),
                  writes=[("kTh", sl)], dma=True)
            for qd in range(4):
                tr.op("sp", lambda h, qd=qd: h.dma_start(
                    out=kth[sl][:, 8 * qd:8 * qd + 8, :], in_=k_v[:, 8 * qd:8 * qd + 8, hd * 128:(hd + 1) * 128]),
                    writes=[("kth", sl, qd)], dma=True)
                tr.op("sp", lambda h, qd=qd: h.dma_start(
                    out=vh[sl][:, 8 * qd:8 * qd + 8, 0:DV], in_=v_v[:, 8 * qd:8 * qd + 8, hd * DV:(hd + 1) * DV]),
                    reads=[("vone", sl)], writes=[("vh", sl, qd)], dma=True)

        KVB = {0: 4, 1: 5, 2: 6, 3: 7}

        def dop(eng, fn, **kw):
            return (eng, lambda: tr.op(eng, fn, **kw))

        def step(hd, i, d):
            sl = hd % 2
            ch = 2 * sl + d
            c = i if d == 0 else NCH - 1 - i
            cprev = c - 1 if d == 0 else c + 1
            col, colp = c * 8 + hd, cprev * 8 + hd
            tok = slice(c * 128, (c + 1) * 128)
            qd = c // 8
            bn, bkv = ("ps", ch), ("ps", KVB[ch])
            psn, pss, pskv = ps[:, ch, 0:DV + 1], ps[:, ch, 260:388], ps[:, KVB[ch], 0:DV + 1]
            fin = i >= NCH // 2
            if fin:
                yield dop("dve", lambda h: h.memset(ssq[:, ch, 0:1], 0.0), writes=[("ssq", ch)])
            yield dop("pe", lambda h: h.matmul(pss, lhsT=kTh[sl][:, tok], rhs=qTh[sl][:, tok], start=True, stop=True),
                        reads=[("kTh", sl), ("qTh", sl)], excl=[bn])
            yield dop("dve", lambda h: h.tensor_tensor(out=smt[:, ch, :], in0=pss, in1=c_mat[:, d, :], op=ALU.mult),
                        reads=[("c_mat",)], excl=[bn], writes=[("smt", ch)])
            yield dop("act", lambda h: h.activation(out=vp[:, ch, 0:DV + 1], in_=vh[sl][:, c, 0:DV + 1], func=AF.Identity,
                                                      scale=er[:, d, col:col + 1]),
                        reads=[("vh", sl, qd), ("vone", sl), ("er", d)], writes=[("vp", ch)])
            yield dop("pe", lambda h: h.matmul(psn, lhsT=smt[:, ch, :], rhs=vp[:, ch, 0:DV + 1], start=True, stop=(i == 0)),
                        reads=[("smt", ch), ("vp", ch)], excl=[bn])
            if i > 0:
                yield dop("pe", lambda h: h.matmul(psn, lhsT=qTh[sl][:, tok], rhs=Cb[:, ch, 0:DV + 1], start=False, stop=True),
                            reads=[("qTh", sl), ("Cb", ch)], excl=[bn])
            yield dop("pe", lambda h: h.matmul(pskv, lhsT=kth[sl][:, c, :], rhs=vp[:, ch, 0:DV + 1], start=True, stop=True),
                        reads=[("kth", sl, qd), ("vp", ch)], excl=[bkv])
            if i == 0:
                yield dop("dve", lambda h: h.tensor_copy(out=Cf[:, ch, 0:DV + 1], in_=pskv), excl=[bkv], writes=[("Cf", ch)])
            else:
                yield dop("dve", lambda h: h.scalar_tensor_tensor(
                    out=Cf[:, ch, 0:DV + 1], in0=Cf[:, ch, 0:DV + 1], scalar=ed[:, d, colp:colp + 1], in1=pskv,
                    op0=ALU.mult, op1=ALU.add),
                    reads=[("Cf", ch), ("ed", d)], excl=[bkv], writes=[("Cf", ch)])
            if i < NCH - 1:
                yield dop("act", lambda h: h.activation(out=Cb[:, ch, 0:DV + 1], in_=Cf[:, ch, 0:DV + 1], func=AF.Identity,
                                                          scale=ed[:, d, col:col + 1]),
                            reads=[("Cf", ch), ("ed", d)], writes=[("Cb", ch)])
            ebi = ebinv[:, d, col:col + 1]
            yield dop("dve", lambda h: h.tensor_scalar(out=dn[:, ch, 0:1], in0=psn[:, DV:DV + 1], scalar1=-1.0, scalar2=ebi,
                                                         op0=ALU.mult, op1=ALU.max),
                        reads=[("ebinv", d)], excl=[bn], writes=[("dn", ch)])
            yield dop("dve", lambda h: h.tensor_tensor(out=dn[:, ch, 1:2], in0=psn[:, DV:DV + 1], in1=dn[:, ch, 0:1], op=ALU.max),
                        reads=[("dn", ch)], excl=[bn], writes=[("dn1", ch)])
            yield dop("dve", lambda h: h.reciprocal(out=dn[:, ch, 4:5], in_=dn[:, ch, 1:2]), reads=[("dn1", ch)], writes=[("fsc", ch)])
            fsc = dn[:, ch, 4:5]
            if not fin:
                yield dop("act", lambda h: h.activation(out=hbuf[:, sl, c, :], in_=psn[:, 0:DV], func=AF.Identity, scale=fsc),
                            reads=[("fsc", ch)], excl=[bn], writes=[("hbuf", sl, c)])
            else:
                yield dop("dve", lambda h: h.scalar_tensor_tensor(out=hs[:, ch, :], in0=psn[:, 0:DV], scalar=fsc,
                                                                    in1=hbuf[:, sl, c, :], op0=ALU.mult, op1=ALU.add),
                            reads=[("fsc", ch), ("hbuf", sl, c)], excl=[bn], writes=[("hs", ch)])
                yield dop("act", lambda h: h.activation(out=junk[:, ch, :], in_=hs[:, ch, :], func=AF.Square,
                                                          accum_out=ssq[:, ch, 0:1]),
                            reads=[("hs", ch), ("ssq", ch)], writes=[("ssq", ch), ("junk", ch)])
                yield dop("act", lambda h: h.activation(out=ssq[:, ch, 1:2], in_=ssq[:, ch, 0:1], func=AF.Sqrt,
                                                          bias=s_eps, scale=1.0 / DV),
                            reads=[("ssq", ch), ("eps",)], writes=[("ssq1", ch)])
                yield dop("dve", lambda h: h.reciprocal(out=ssq[:, ch, 2:3], in_=ssq[:, ch, 1:2]),
                            reads=[("ssq1", ch)], writes=[("ssq2", ch)])
                yield dop("dve", lambda h: h.scalar_tensor_tensor(
                    out=hmb[:, ch, :], in0=hs[:, ch, :], scalar=ssq[:, ch, 2:3], in1=mnw_sb[:, hd * DV:(hd + 1) * DV],
                    op0=ALU.mult, op1=ALU.mult),
                    reads=[("hs", ch), ("ssq2", ch), ("mnw",)], writes=[("hmb", ch)])
                tb = [bn, bkv]
                tps = [ps[:, ch, 260:388], ps[:, KVB[ch], 260:388]]
                for vhf in range(2):
                    yield dop("pe", lambda h, vhf=vhf: h.transpose(
                        out=tps[vhf], in_=hmb[:, ch, vhf * 128:(vhf + 1) * 128], identity=c_mat[:, 3, :]),
                        reads=[("hmb", ch), ("c_mat",)], excl=[tb[vhf]])
                for vhf in range(2):
                    yield dop("act", lambda h, vhf=vhf: h.activation(
                        out=hmT_st[:, sl, vhf, tok], in_=tps[vhf], func=AF.Copy),
                        excl=[tb[vhf]], writes=[("hmT_st", sl, c, vhf)])

        for hp in range(H // NHP):
            heads = [hp * NHP + j for j in range(NHP)]
            for hd in heads:
                load_head(hd)
            def chain(hd, d):
                for i in range(NCH):
                    yield from step(hd, i, d)

            SKEW = 5
            gens = [[chain(hd, d), SKEW * k] for k, (hd, d) in enumerate((hd, d) for d in range(2) for hd in heads)]
            rounds = 0
            while gens:
                rounds += 1
                if rounds % 60 == 0 and ada_state[0] < NHB:
                    j = ada_state[0]
                    ada_mm(j)
                    if j + 2 < NHB:
                        ada_load(j + 2)
                    conv_some(2)
                    ada_state[0] += 1
                for g in list(gens):
                    if g[1] > 0:
                        g[1] -= 1
                        continue
                    try:
                        eng, thunk = next(g[0])
                        thunk()
                    except StopIteration:
                        gens.remove(g)
            for hd in heads:
                tr.op("sp", lambda h, hd=hd: h.dma_start(out=hm_v[:, 2 * hd:2 * hd + 2, :], in_=hmT_st[:, hd % 2]),
                      reads=[("hmT_st", hd % 2, c, v_) for c in range(NCH) for v_ in range(2)], writes=[("hm_s", hd)], dma=True)
        while ada_state[0] < NHB:
            j = ada_state[0]
            ada_mm(j)
            if j + 2 < NHB:
                ada_load(j + 2)
            ada_state[0] += 1
        conv_some(len(conv_q))
        tr.op("dve", lambda h: h.tensor_tensor(out=s_ada[:, 32:96], in0=ps_ada2, in1=s_bada[:, 32:96], op=ALU.add),
              reads=[("bada",)], excl=[("ps", 4)], writes=[("ada2",)])
        tr.op("dve", lambda h: h.scalar_tensor_tensor(out=s_a2, in0=s_ada[:, 64:80], scalar=1.0, in1=s_n2w,
                                                       op0=ALU.add, op1=ALU.mult),
              reads=[("ada2",), ("n2w",)], writes=[("a2",)])
        if dbg:
            tr.op("sp", lambda h: h.dma_start(out=ada_dbg, in_=s_ada), reads=[("ada2",)], writes=[("ada_dbg",)], dma=True)
        tr.barrier()

    if "C" in phases:
        PC_BLOCKS = WIN["cb"] + WIN["o"]
        for j in range(4):
            PC_BLOCKS += [WIN["bgc"][j], WCO[j], WIN["bgm"][j], WMO[j]]
        PC_BLOCKS += WOO
        for j in range(11):
            PC_BLOCKS += [WGG[j], WGU[j]]
        for j in range(4):
            PC_BLOCKS += WDN[j]
        alC = Alloc(base_persist)
        xs = alC.get("xsC", [128, KC, T], F32)
        hT = alC.get("hTC", [128, KC, T], BF16)
        actT = alC.get("actT", [128, FKC, T], BF16)
        hmT = alC.get("hmT", [128, KC, T], BF16)
        pl = alC.get("pl", [128, 8, T + 2], F32)
        cvt = alC.get("cvt", [128, 2, T], F32)
        rstd = alC.get("rstdC", [128, T], F32)
        ntmp = alC.get("ntmp", [128, 2, T], F32)
        G1 = alC.get("G1", [128, 4, T], F32)
        mg = alC.get("mg", [128, 4, T], F32)
        tmpF = alC.get("tmpF", [128, 2, T], F32)
        ws = WStream(alC)
        merged = actT[:, 0:16, :]
        uT = actT[:, 16:24, :]
        sq = actT[:, 24:40, :]
        xT_v = xT.rearrange("(kc p) t -> p kc t", p=128)
        oT_v = outT.rearrange("(kc p) t -> p kc t", p=128)
        hmr_v = hm_s.rearrange("(kc p) t -> p kc t", p=128)
        tf = [0]
        cvn = [0]

        def tileC(it):
            t0 = it * T
            for half in range(2):
                tr.op("sp", lambda h, half=half: h.dma_start(
                    out=xs[:, 8 * half:8 * half + 8, :], in_=xT_v[:, 8 * half:8 * half + 8, t0:t0 + T]),
                    writes=[("xs", kc) for kc in range(8 * half, 8 * half + 8)], dma=True)
            tr.op("sp", lambda h: h.dma_start(out=pl[:], in_=p_v[:, :, t0:t0 + T + 2]), writes=[("pl", ch) for ch in range(8)], dma=True)
            tr.op("sp", lambda h: h.dma_start(out=hmT[:], in_=hmr_v[:, :, t0:t0 + T]), writes=[("hmT", kc) for kc in range(KC)], dma=True)
            sqk = lambda kc: ("act", 24 + kc)
            norm_stats(xs, sq, rstd, "xs", sqk)
            norm_apply(xs, "xs", rstd, s_a1, ("a1",), b1, ("ada",), hT, ntmp, False)
            tasks = []

            def conv_chunk(ch):
                slot = cvn[0] % 2
                cvn[0] += 1
                cv = cvt[:, slot, :]
                tr.op("dve", lambda h: h.tensor_scalar(out=cv, in0=pl[:, ch, 0:T], scalar1=s_convw[:, ch:ch + 1],
                                                        scalar2=None, op0=ALU.mult),
                      reads=[("pl", ch), ("convw",)], writes=[("cvt", slot)])
                for tap in (1, 2):
                    tr.op("dve", lambda h, tap=tap: h.scalar_tensor_tensor(
                        out=cv, in0=pl[:, ch, tap:tap + T], scalar=s_convw[:, 8 * tap + ch:8 * tap + ch + 1], in1=cv,
                        op0=ALU.mult, op1=ALU.add),
                        reads=[("pl", ch), ("convw",), ("cvt", slot)], writes=[("cvt", slot)])
                return cv, ("cvt", slot)

            def t_cb(j):
                def f(wv, wk):
                    def ev(c, pt, pk):
                        cv, ck = conv_chunk(4 * j + c)
                        tr.op("dve", lambda h: h.tensor_tensor(out=uT[:, 4 * j + c, :], in0=pt, in1=cv, op=ALU.mult),
                              reads=[pk, ck], writes=[("act", 16 + 4 * j + c)])
                    fm_tiles(wv, wk, hT, "hT", KC, ev)
                return f

            def t_o(j):
                def f(wv, wk):
                    def ev(c, pt, pk):
                        slot = tf[0] % 2
                        tf[0] += 1
                        tr.op("act", lambda h: h.activation(out=tmpF[:, slot, :], in_=pt, func=AF.Sigmoid),
                              reads=[pk], writes=[("tmpF", slot)])
                        tr.op("dve", lambda h: h.tensor_tensor(out=hmT[:, 4 * j + c, :], in0=tmpF[:, slot, :],
                                                               in1=hmT[:, 4 * j + c, :], op=ALU.mult),
                              reads=[("tmpF", slot), ("hmT", 4 * j + c)], writes=[("hmT", 4 * j + c)])
                    fm_tiles(wv, wk, hT, "hT", KC, ev)
                return f

            def t_bg(j):
                def f(wv, wk):
                    def ev(c, pt, pk):
                        tr.op("act", lambda h: h.activation(out=G1[:, c, :], in_=pt, func=AF.Sigmoid),
                              reads=[pk], writes=[("G1", c)])
                    fm_tiles(wv, wk, hT, "hT", KC, ev)
                return f

            def t_co(j):
                def f(wv, wk):
                    def ev(c, pt, pk):
                        tr.op("dve", lambda h: h.tensor_tensor(out=mg[:, c, :], in0=pt, in1=G1[:, c, :], op=ALU.mult),
                              reads=[pk, ("G1", c)], writes=[("mg", c)])
                    fm_tiles(wv, wk, uT, lambda kc: ("act", 16 + kc), 8, ev)
                return f

            def t_mo(j):
                def f(wv, wk):
                    def ev(c, pt, pk):
                        slot = tf[0] % 2
                        tf[0] += 1
                        tr.op("dve", lambda h: h.tensor_tensor(out=tmpF[:, slot, :], in0=pt, in1=G1[:, c, :], op=ALU.mult),
                              reads=[pk, ("G1", c)], writes=[("tmpF", slot)])
                        tr.op("pool", lambda h: h.tensor_tensor(out=merged[:, 4 * j + c, :], in0=tmpF[:, slot, :],
                                                                in1=mg[:, c, :], op=ALU.add),
                              reads=[("tmpF", slot), ("mg", c)], writes=[("act", 4 * j + c)])
                    fm_tiles(wv, wk, hmT, "hmT", KC, ev)
                return f

            def t_res(j, act_t, akey, gcol):
                def f(wv, wk):
                    def ev(c, pt, pk):
                        e = 4 * j + c
                        tr.op("dve", lambda h: h.scalar_tensor_tensor(
                            out=xs[:, e, :], in0=pt, scalar=gcol[:, e:e + 1], in1=xs[:, e, :], op0=ALU.mult, op1=ALU.add),
                            reads=[pk, ("xs", e), ("ada",)], writes=[("xs", e)])
                    fm_tiles(wv, wk, act_t, akey, KC, ev)
                return f

            tasks += [(WIN["cb"][j], t_cb(j)) for j in range(2)]
            tasks += [(WIN["o"][j], t_o(j)) for j in range(4)]
            for j in range(4):
                tasks += [(WIN["bgc"][j], t_bg(j)), (WCO[j], t_co(j)), (WIN["bgm"][j], t_bg(j)), (WMO[j], t_mo(j))]
            tasks += [(WOO[j], t_res(j, merged, lambda kc: ("act", kc), g1)) for j in range(4)]
            run_blocks(ws, tasks)

            norm_stats(xs, sq, rstd, "xs", sqk)
            norm_apply(xs, "xs", rstd, s_a2, ("a2",), b2, ("ada",), hT, ntmp, False)
            tasks = []

            def t_gg(j):
                def f(wv, wk):
                    def ev(c, pt, pk):
                        tr.op("act", lambda h: h.activation(out=G1[:, c, :], in_=pt, func=AF.Silu),
                              reads=[pk], writes=[("G1", c)])
                    fm_tiles(wv, wk, hT, "hT", KC, ev)
                return f

            def t_gu(j):
                def f(wv, wk):
                    def ev(c, pt, pk):
                        tr.op("dve", lambda h: h.tensor_tensor(out=actT[:, 4 * j + c, :], in0=pt, in1=G1[:, c, :], op=ALU.mult),
                              reads=[pk, ("G1", c)], writes=[("act", 4 * j + c)])
                    fm_tiles(wv, wk, hT, "hT", KC, ev)
                return f

            dn_banks = {}

            def t_dn(j, q):
                def f(wv, wk):
                    if q == 0:
                        dn_banks[j] = [bank() for _ in range(4)]
                    for c in range(4):
                        bk = dn_banks[j][c]
                        for kc in range(11):
                            tr.op("pe", lambda h, c=c, kc=kc, bk=bk: h.matmul(
                                ps[:, bk, :], lhsT=wv[:, kc, c * 128:(c + 1) * 128], rhs=actT[:, 11 * q + kc, :],
                                start=(q == 0 and kc == 0), stop=(q == 3 and kc == 10)),
                                reads=[wk, ("act", 11 * q + kc)], writes=[("ps", bk)])
                        if q == 3:
                            e = 4 * j + c
                            tr.op("dve", lambda h, e=e, bk=bk: h.scalar_tensor_tensor(
                                out=xs[:, e, :], in0=ps[:, bk, :], scalar=g2[:, e:e + 1], in1=xs[:, e, :],
                                op0=ALU.mult, op1=ALU.add),
                                reads=[("ps", bk), ("xs", e), ("ada",)], writes=[("xs", e)])
                return f

            for j in range(11):
                tasks += [(WGG[j], t_gg(j)), (WGU[j], t_gu(j))]
            for j in range(4):
                tasks += [(WDN[j][q], t_dn(j, q)) for q in range(4)]
            run_blocks(ws, tasks)

            norm_stats(xs, sq, rstd, "xs", sqk)
            for kc in range(KC):
                tr.op("dve", lambda h, kc=kc: h.scalar_tensor_tensor(
                    out=xs[:, kc, :], in0=xs[:, kc, :], scalar=s_fnw[:, kc:kc + 1], in1=rstd[:], op0=ALU.mult, op1=ALU.mult),
                    reads=[("xs", kc), ("rstd",), ("fnw",)], writes=[("xs", kc)])
            for half in range(2):
                tr.op("sp", lambda h, half=half: h.dma_start(
                    out=oT_v[:, 8 * half:8 * half + 8, t0:t0 + T], in_=xs[:, 8 * half:8 * half + 8, :]),
                    reads=[("xs", kc) for kc in range(8 * half, 8 * half + 8)], writes=[("out", it, half)], dma=True)

        for it in range(NT):
            tileC(it)

    with nc.Block() as block:
        tr.emit(block)
    return nc


def make_inputs(inputs, b):
    f = np.float32
    x = np.asarray(inputs["x"], f)
    d = {}
    d["xT"] = np.ascontiguousarray(x[b].T)
    d["ccol"] = np.ascontiguousarray(np.asarray(inputs["c"], f)[b].reshape(KC, 128).T)
    d["w_ada"] = np.ascontiguousarray(np.asarray(inputs["w_ada"], f)[0])
    d["bada"] = np.ascontiguousarray(np.asarray(inputs["b_ada"], f)[0].reshape(96, 128).T)
    d["n1w"] = np.ascontiguousarray(np.asarray(inputs["norm1_w"], f)[0].reshape(KC, 128).T)
    d["n2w"] = np.ascontiguousarray(np.asarray(inputs["norm2_w"], f)[0].reshape(KC, 128).T)
    d["fnw"] = np.ascontiguousarray(np.asarray(inputs["final_norm_w"], f).reshape(KC, 128).T)
    d["w_in"] = np.ascontiguousarray(np.asarray(inputs["w_in_mix"], f)[0])
    d["convw"] = np.ascontiguousarray(np.asarray(inputs["conv_w"], f)[0].reshape(3, 8, 128).transpose(2, 0, 1).reshape(128, 24))
    d["gbias"] = np.ascontiguousarray(np.broadcast_to(np.asarray(inputs["mlstm_gate_bias"], f)[0][None, :], (128, 32)))
    d["mnw"] = np.ascontiguousarray(np.broadcast_to(np.asarray(inputs["mlstm_norm_w"], f)[0][None, :], (128, D)))
    d["w_co"] = np.ascontiguousarray(np.asarray(inputs["w_conv_out"], f)[0])
    d["w_mo"] = np.ascontiguousarray(np.asarray(inputs["w_mlstm_out"], f)[0])
    d["w_oo"] = np.ascontiguousarray(np.asarray(inputs["w_o"], f)[0])
    d["w_gu"] = np.ascontiguousarray(np.asarray(inputs["w_gate_up"], f)[0])
    d["w_dn"] = np.ascontiguousarray(np.asarray(inputs["w_down"], f)[0])
    cm = np.zeros((128, 4, 128), f)
    i = np.arange(128)
    cm[:, 0, :] = (i[:, None] <= i[None, :])
    cm[:, 1, :] = (i[:, None] >= i[None, :])
    cm[:, 2, :] = 1.0 / D
    cm[:, 3, :] = np.eye(128)
    d["cmat"] = cm
    return d


_NC_CACHE = {}


def kernel(**inputs):
    if "nc" not in _NC_CACHE:
        _NC_CACHE["nc"] = build_program()
    nc = _NC_CACHE["nc"]
    in_maps = [make_inputs(inputs, b) for b in range(NCORES)]
    res = run_bass_kernel_spmd(nc, in_maps, core_ids=list(range(NCORES)))
    out = np.stack([np.ascontiguousarray(res.results[b]["outT"].T) for b in range(NCORES)], axis=0)
    return out.astype(np.float32)
```

```python
import numpy as np
import concourse.bass as bass
import concourse.mybir as mybir
from concourse.bass_utils import run_bass_kernel_spmd

F32 = mybir.dt.float32
BF16 = mybir.dt.bfloat16
AF = mybir.ActivationFunctionType
ALU = mybir.AluOpType

D = 2048
S = 4096
NCORES = 8
T = 512
NT = S // T
KC = D // 128
EPS = 1e-6
H = 8
DK = 128
DV = 256
L = 128
NCH = S // L
FF = 5632
FKC = FF // 128
IN_COLS = 13344
OFF_CB, OFF_CC, OFF_CX, OFF_Q, OFF_K, OFF_V, OFF_O, OFF_G, OFF_BG = 0, 1024, 2048, 3072, 4096, 5120, 7168, 9216, 9248


class Op:
    __slots__ = ("eng", "fn", "deps", "is_dma", "sig", "count", "dsem", "dval", "idx", "xprev")


class Tracker:
    ENGS = ("pe", "act", "dve", "pool", "sp")

    def __init__(self, nc, n_dma_sems):
        self.nc = nc
        self.ops = {e: [] for e in self.ENGS}
        self.last_writer = {}
        self.readers = {}
        self.esem = {e: nc.alloc_semaphore("c_" + e) for e in ("pe", "act", "dve", "pool")}
        self.dsems = {q: [nc.alloc_semaphore("d_%s%d" % (q, i)) for i in range(n)] for q, n in n_dma_sems.items()}
        self.drr = {q: 0 for q in n_dma_sems}
        self.duse = {}
        self.dlast = {}

    def op(self, eng, fn, reads=(), writes=(), dma=False, excl=()):
        o = Op()
        o.eng, o.fn, o.is_dma, o.sig, o.count = eng, fn, dma, False, 0
        o.idx = len(self.ops[eng])
        deps = {}
        for k in excl:
            w = self.last_writer.get(k)
            if w is not None and w.eng != eng:
                deps[id(w)] = w
            elif w is not None and w.eng == eng:
                pw = getattr(w, "xprev", {}).get(k)
                if pw is not None and pw.eng != eng:
                    deps[id(pw)] = pw
        for k in reads:
            w = self.last_writer.get(k)
            if w is not None:
                deps[id(w)] = w
        for k in writes:
            w = self.last_writer.get(k)
            if w is not None:
                deps[id(w)] = w
            for r in self.readers.get(k, {}).values():
                for rr in r:
                    deps[id(rr)] = rr
        dl = []
        for d in deps.values():
            if d is o:
                continue
            if (not d.is_dma) and d.eng == eng and not dma:
                if eng == "pe":
                    continue
            dl.append(d)
        o.deps = dl
        if dma:
            sems = self.dsems[eng]
            s = sems[self.drr[eng] % len(sems)]
            self.drr[eng] += 1
            self.duse[s] = self.duse.get(s, 0) + 1
            o.dsem, o.dval = s, 16 * self.duse[s]
            prev = self.dlast.get(s)
            if prev is not None:
                o.deps.append(prev)
            self.dlast[s] = o
        if excl:
            o.xprev = {}
            for k in excl:
                w = self.last_writer.get(k)
                if w is not None:
                    o.xprev[k] = w if w.eng != eng else getattr(w, "xprev", {}).get(k)
                self.last_writer[k] = o
                self.readers[k] = {}
        for k in writes:
            self.last_writer[k] = o
            self.readers[k] = {}
        for k in reads:
            rd = self.readers.setdefault(k, {})
            if dma:
                rd.setdefault("dma", []).append(o)
            else:
                rd[eng] = [o]
        self.ops[eng].append(o)
        return o

    def barrier(self):
        lasts = []
        for e in self.ENGS:
            for o in reversed(self.ops[e]):
                if not o.is_dma and o.fn is not None:
                    lasts.append(o)
                    break
        dl = [x for x in self.dlast.values() if x.eng != "pool"]
        for e in self.ENGS:
            o = Op()
            o.eng, o.fn, o.is_dma, o.sig, o.count = e, None, False, False, 0
            o.idx = len(self.ops[e])
            o.deps = [x for x in lasts if x.eng != e] + dl
            self.ops[e].append(o)
        self.last_writer = {k: v for k, v in self.last_writer.items() if k[0] == "wscr"}
        self.readers = {}

    def emit(self, block):
        for e in self.ENGS:
            for o in self.ops[e]:
                for d in o.deps:
                    if not d.is_dma:
                        d.sig = True
        for e in self.ENGS:
            c = 0
            for o in self.ops[e]:
                if o.sig:
                    c += 1
                    o.count = c
        fin = Op()
        fin.eng, fin.fn, fin.is_dma, fin.sig, fin.count = "sp", None, False, False, 0
        fin.deps = list(self.dlast.values())
        self.ops["sp"].append(fin)

        def run(e, h):
            waited = {}
            for o in self.ops[e]:
                for d in o.deps:
                    if d.is_dma:
                        sem, val = d.dsem, d.dval
                    else:
                        sem, val = self.esem[d.eng], d.count
                    if waited.get(sem.num, 0) >= val:
                        continue
                    waited[sem.num] = val
                    h.wait_ge(sem, val)
                if o.fn is None:
                    continue
                ins = o.fn(h)
                if o.is_dma:
                    ins.then_inc(o.dsem, 16)
                elif o.sig:
                    ins.then_inc(self.esem[e], 1)

        @block.tensor
        def _(h):
            run("pe", h)

        @block.scalar
        def _(h):
            run("act", h)

        @block.vector
        def _(h):
            run("dve", h)

        @block.gpsimd
        def _(h):
            run("pool", h)

        @block.sync
        def _(h):
            run("sp", h)


def build_program(dbg=False, phases="0ABC"):
    nc = bass.Bass("TRN2", target_bir_lowering=False)
    tr = Tracker(nc, {"sp": 24, "pool": 12})

    def din(name, shape, dt=F32):
        return nc.dram_tensor(name, list(shape), dt, kind="ExternalInput").ap()

    def dscr(name, shape, dt):
        return nc.dram_tensor(name, list(shape), dt, kind=("ExternalOutput" if dbg else "Internal")).ap()

    xT = din("xT", [D, S])
    ccol = din("ccol", [128, KC])
    w_ada = din("w_ada", [D, 6 * D])
    bada = din("bada", [128, 96])
    n1w = din("n1w", [128, KC])
    n2w = din("n2w", [128, KC])
    fnw = din("fnw", [128, KC])
    w_in = din("w_in", [D, IN_COLS])
    convw = din("convw", [128, 24])
    gbias = din("gbias", [128, 32])
    mnw = din("mnw", [128, D])
    w_co = din("w_co", [1024, D])
    w_mo = din("w_mo", [D, D])
    w_oo = din("w_oo", [D, D])
    w_gu = din("w_gu", [D, 2 * FF])
    w_dn = din("w_dn", [FF, D])
    cmat = din("cmat", [128, 4, 128])
    outT = nc.dram_tensor("outT", [D, S], F32, kind="ExternalOutput").ap()

    qT_s = dscr("qT_s", [H * DK, S], BF16)
    kT_s = dscr("kT_s", [H * DK, S], BF16)
    k_s = dscr("k_s", [S, H * DK], BF16)
    v_s = dscr("v_s", [S, H * DV], BF16)
    g_s = dscr("g_s", [S, 32], F32)
    p_s = dscr("p_s", [1024, S + 2], F32)
    hm_s = dscr("hm_s", [H * DV, S], BF16)
    ada_dbg = dscr("ada_dbg", [128, 96], F32) if dbg else None

    class WB:
        pass
    wblocks = []

    def mk_block(name, src, k0, kn, c0, width):
        b = WB()
        b.name, b.src, b.k0, b.kn, b.c0, b.width = name, src, k0, kn, c0, width
        b.scr = nc.dram_tensor("wb_" + name, [128, kn * width], BF16, kind="Internal").ap()
        b.converted = False
        wblocks.append(b)
        return b

    def convert(b):
        if b.converted:
            return
        b.converted = True
        srcv = b.src.rearrange("(kc p) e -> p kc e", p=128)
        dstv = b.scr.rearrange("p (kc e) -> p kc e", e=b.width)
        step = 4
        for i, k in enumerate(range(0, b.kn, step)):
            kk = min(step, b.kn - k)
            tr.op("pool", lambda h, k=k, kk=kk: h.dma_start(
                out=dstv[:, k:k + kk, :], in_=srcv[:, b.k0 + k:b.k0 + k + kk, b.c0:b.c0 + b.width]),
                reads=(), writes=[("wscr", b.name, i)], dma=True)
        b.npieces = (b.kn + step - 1) // step

    WIN = {}
    for nm, off, n in (("cb", OFF_CB, 2), ("cc", OFF_CC, 2), ("cx", OFF_CX, 2), ("q", OFF_Q, 2), ("k", OFF_K, 2),
                       ("v", OFF_V, 4), ("o", OFF_O, 4), ("bgc", OFF_BG, 4), ("bgm", OFF_BG + D, 4)):
        WIN[nm] = [mk_block("%s%d" % (nm, j), w_in, 0, KC, off + 512 * j, 512) for j in range(n)]
    WG = mk_block("g", w_in, 0, KC, OFF_G, 32)
    WCO = [mk_block("co%d" % j, w_co, 0, 8, 512 * j, 512) for j in range(4)]
    WMO = [mk_block("mo%d" % j, w_mo, 0, KC, 512 * j, 512) for j in range(4)]
    WOO = [mk_block("oo%d" % j, w_oo, 0, KC, 512 * j, 512) for j in range(4)]
    WGG = [mk_block("gg%d" % j, w_gu, 0, KC, 512 * j, 512) for j in range(11)]
    WGU = [mk_block("gu%d" % j, w_gu, 0, KC, FF + 512 * j, 512) for j in range(11)]
    WDN = [[mk_block("dn%d_%d" % (j, q), w_dn, 11 * q, 11, 512 * j, 512) for q in range(4)] for j in range(4)]

    class Alloc:
        def __init__(self, base):
            self.off = base

        def get(self, name, shape, dt):
            nbytes = int(np.prod(shape[1:])) * (2 if dt == BF16 else 4)
            nbytes = (nbytes + 63) // 64 * 64
            t = nc.alloc_sbuf_tensor_at(name, list(shape), dt, offset=self.off)
            self.off += nbytes
            assert self.off <= 229120, (name, self.off)
            return t

    al = Alloc(16640)
    c_mat = al.get("c_mat", [128, 4, 128], F32)
    c_bf = al.get("c_bf", [128, 4, 128], BF16)
    sm_f = al.get("sm_f", [128, 8 * KC + 96 + 24 + 32 + 16], F32)
    o_ = 0
    def smv(n):
        nonlocal o_
        v = sm_f[:, o_:o_ + n]
        o_ += n
        return v
    s_ccol = smv(KC); s_n1w = smv(KC); s_n2w = smv(KC); s_fnw = smv(KC)
    s_a1 = smv(KC); s_a2 = smv(KC); s_tmp16 = smv(KC); s_spare = smv(KC)
    s_ada = smv(96); s_convw = smv(24); s_gbias = smv(32); s_bada_unused = smv(16)
    s_bada = al.get("s_bada", [128, 96], F32)
    s_epst = al.get("s_epst", [128, 8], F32)
    s_eps = s_epst[:, 0:1]
    cact = al.get("cact", [128, KC], BF16)
    base_persist = al.off

    ps = nc.alloc_psum_tensor("ps", [128, 8, 512], F32)
    triF, triB, onesm, ident = (c_bf[:, i, :] for i in range(4))

    rot = [0]

    def bank():
        b = rot[0] % 6
        rot[0] += 1
        return b

    def sp_load(dst, src, wkeys, rkeys=()):
        return tr.op("sp", lambda h: h.dma_start(out=dst, in_=src), reads=list(rkeys), writes=list(wkeys), dma=True)

    sp_load(c_mat[:], cmat, [("c_mat",)])
    sp_load(sm_f[:, 0:KC], ccol, [("ccol",)])
    sp_load(s_n1w, n1w, [("n1w",)])
    sp_load(s_n2w, n2w, [("n2w",)])
    sp_load(s_fnw, fnw, [("fnw",)])
    sp_load(s_convw, convw, [("convw",)])
    sp_load(s_gbias, gbias, [("gbias",)])
    sp_load(s_bada[:], bada, [("bada",)])
    tr.op("dve", lambda h: h.tensor_copy(out=c_bf[:], in_=c_mat[:]), reads=[("c_mat",)], writes=[("c_bf",)])
    tr.op("dve", lambda h: h.memset(s_epst[:], EPS), writes=[("eps",)])
    tr.op("act", lambda h: h.activation(out=cact[:], in_=s_ccol, func=AF.Silu), reads=[("ccol",)], writes=[("cact",)])

    al0 = Alloc(base_persist)
    wada = [al0.get("wada%d" % i, [128, KC, 512], BF16) for i in range(2)]
    w_ada_v = w_ada.rearrange("(kc p) e -> p kc e", p=128)
    ps_ada = ps[:, 6, 0:96]
    NB0 = 8
    for blk in range(NB0):
        wt = wada[blk % 2]
        for qd in range(4):
            tr.op("pool", lambda h, wt=wt, qd=qd, blk=blk: h.dma_start(
                out=wt[:, 4 * qd:4 * qd + 4, :], in_=w_ada_v[:, 4 * qd:4 * qd + 4, 512 * blk:512 * blk + 512]),
                writes=[("wada", blk % 2, qd)], dma=True)
        for sub in range(4):
            col = blk * 4 + sub
            for kc in range(KC):
                tr.op("pe", lambda h, wt=wt, sub=sub, kc=kc, col=col: h.matmul(
                    ps_ada[:, col:col + 1], lhsT=wt[:, kc, sub * 128:(sub + 1) * 128], rhs=cact[:, kc:kc + 1],
                    start=(kc == 0), stop=(kc == KC - 1)),
                    reads=[("wada", blk % 2, kc // 4), ("cact",)], writes=[("psada",)])
    tr.op("dve", lambda h: h.tensor_tensor(out=s_ada[:, 0:32], in0=ps_ada[:, 0:32], in1=s_bada[:, 0:32], op=ALU.add),
          reads=[("psada",), ("bada",)], writes=[("ada",)])
    b1 = s_ada[:, 0:16]; g1 = s_ada[:, 32:48]; b2 = s_ada[:, 48:64]; g2 = s_ada[:, 80:96]
    tr.op("dve", lambda h: h.scalar_tensor_tensor(out=s_a1, in0=s_ada[:, 16:32], scalar=1.0, in1=s_n1w,
                                                   op0=ALU.add, op1=ALU.mult),
          reads=[("ada",), ("n1w",)], writes=[("a1",)])

    PA_BLOCKS = WIN["cc"] + WIN["cx"] + WIN["q"] + WIN["k"] + WIN["v"] + [WG]
    for b in PA_BLOCKS:
        convert(b)

    zt = al0.get("zt", [128, 8], F32)
    tr.op("dve", lambda h: h.memset(zt[:], 0.0), writes=[("zt",)])
    p_v = p_s.rearrange("(c p) t -> p c t", p=128)
    tr.op("sp", lambda h: h.dma_start(out=p_v[:, :, 0:1], in_=zt[:, 0:8].rearrange("p (c o) -> p c o", o=1), allow_slow_non_contiguous=True),
          reads=[("zt",)], writes=[("p_pad0",)], dma=True)
    tr.op("sp", lambda h: h.dma_start(out=p_v[:, :, S + 1:S + 2], in_=zt[:, 0:8].rearrange("p (c o) -> p c o", o=1), allow_slow_non_contiguous=True),
          reads=[("zt",)], writes=[("p_pad1",)], dma=True)

    tr.barrier()

    class WStream:
        def __init__(self, alloc, nslots=2):
            self.slots = [alloc.get("wsl%d" % i, [128, KC * 512], BF16) for i in range(nslots)]
            self.n = 0

        def load(self, b):
            i = self.n % len(self.slots)
            self.n += 1
            sl = self.slots[i]
            dst = sl[:, 0:b.kn * b.width]
            tr.op("sp", lambda h: h.dma_start(out=dst, in_=b.scr),
                  reads=[("wscr", b.name, j) for j in range(b.npieces)], writes=[("wsl", i)], dma=True)
            return (sl[:, 0:b.kn * b.width].rearrange("p (kc e) -> p kc e", e=b.width), ("wsl", i))

    def run_blocks(ws, tasks):
        pend = [ws.load(tasks[0][0])] if tasks else []
        for i, (b, fn) in enumerate(tasks):
            if i + 1 < len(tasks):
                pend.append(ws.load(tasks[i + 1][0]))
            wv, wk = pend.pop(0)
            fn(wv, wk)

    def norm_stats(xs, sq, rstd, xkey, sqkey=lambda kc: ("sq", kc)):
        for kc in range(KC):
            tr.op("act", lambda h, kc=kc: h.activation(out=sq[:, kc, :], in_=xs[:, kc, :], func=AF.Square),
                  reads=[(xkey, kc)], writes=[sqkey(kc)])
        for kc in range(KC):
            tr.op("pe", lambda h, kc=kc: h.matmul(ps[:, 6, :], lhsT=onesm, rhs=sq[:, kc, :],
                                                  start=(kc == 0), stop=(kc == KC - 1)),
                  reads=[sqkey(kc), ("c_bf",)], writes=[("ps", 6)])
        tr.op("act", lambda h: h.activation(out=rstd[:], in_=ps[:, 6, :], func=AF.Sqrt, bias=s_eps, scale=1.0),
              reads=[("ps", 6), ("eps",)], writes=[("rstd0",)])
        tr.op("dve", lambda h: h.reciprocal(out=rstd[:], in_=rstd[:]),
              reads=[("rstd0",)], writes=[("rstd",)])

    def norm_apply(xs, xkey, rstd, a, akey, b, bkey, hT, tmp, inplace):
        for kc in range(KC):
            if inplace:
                dst, dkey = xs[:, kc, :], (xkey, kc)
            else:
                dst, dkey = tmp[:, kc % 2, :], ("ntmp", kc % 2)
            tr.op("dve", lambda h, kc=kc, dst=dst: h.scalar_tensor_tensor(
                out=dst, in0=xs[:, kc, :], scalar=a[:, kc:kc + 1], in1=rstd[:], op0=ALU.mult, op1=ALU.mult),
                reads=[(xkey, kc), ("rstd",), akey], writes=[dkey])
            tr.op("act", lambda h, kc=kc, dst=dst: h.activation(
                out=hT[:, kc, :], in_=dst, func=AF.Identity, bias=b[:, kc:kc + 1], scale=1.0),
                reads=[dkey, bkey], writes=[("hT", kc)])

    def fm_tiles(wv, wk, act_t, akey, kn, fn_evac, ntiles=4):
        for c in range(ntiles):
            bk = bank()
            for kc in range(kn):
                tr.op("pe", lambda h, c=c, kc=kc, bk=bk: h.matmul(
                    ps[:, bk, :], lhsT=wv[:, kc, c * 128:(c + 1) * 128], rhs=act_t[:, kc, :],
                    start=(kc == 0), stop=(kc == kn - 1)),
                    reads=[wk, (akey(kc) if callable(akey) else (akey, kc))], writes=[("ps", bk)])
            fn_evac(c, ps[:, bk, :], ("ps", bk))

    if "A" in phases:
        alA = Alloc(base_persist)
        xs = alA.get("xsA", [128, KC, T], F32)
        hT = alA.get("hTA", [128, KC, T], BF16)
        sq = alA.get("sqA", [128, KC, T], BF16)
        rstd = alA.get("rstdA", [128, T], F32)
        ws = WStream(alA)
        st_q = alA.get("st_q", [128, 8, T], BF16)
        st_k = alA.get("st_k", [128, 8, T], BF16)
        st_kt = alA.get("st_kt", [128, 4, 1024], BF16)
        st_vt = alA.get("st_vt", [128, 4, 2048], BF16)
        st_g = alA.get("st_g", [128, 4, 32], F32)
        st_cc = alA.get("st_cc", [128, 8, T], F32)
        st_p = alA.get("st_p", [128, 8, T], F32)
        xT_v = xT.rearrange("(kc p) t -> p kc t", p=128)
        qT_v = qT_s.rearrange("(h p) t -> p h t", p=128)
        kT_v = kT_s.rearrange("(h p) t -> p h t", p=128)
        KS = float(DK ** -0.5)

        def tileA(it):
            t0 = it * T
            for half in range(2):
                tr.op("sp", lambda h, half=half, t0=t0: h.dma_start(
                    out=xs[:, 8 * half:8 * half + 8, :], in_=xT_v[:, 8 * half:8 * half + 8, t0:t0 + T]),
                    writes=[("xs", kc) for kc in range(8 * half, 8 * half + 8)], dma=True)
            norm_stats(xs, sq, rstd, "xs")
            norm_apply(xs, "xs", rstd, s_a1, ("a1",), b1, ("ada",), hT, None, True)
            tasks = []

            def t_cc(j):
                def f(wv, wk):
                    def ev(c, pt, pk):
                        tr.op("act", lambda h: h.activation(out=st_cc[:, 4 * j + c, :], in_=pt, func=AF.Copy),
                              reads=[pk], writes=[("st_cc", 4 * j + c)])
                    fm_tiles(wv, wk, hT, "hT", KC, ev)
                return f

            def t_cx(j):
                def f(wv, wk):
                    def ev(c, pt, pk):
                        tr.op("dve", lambda h: h.tensor_tensor(out=st_p[:, 4 * j + c, :], in0=pt,
                                                               in1=st_cc[:, 4 * j + c, :], op=ALU.mult),
                              reads=[pk, ("st_cc", 4 * j + c)], writes=[("st_p", 4 * j + c)])
                    fm_tiles(wv, wk, hT, "hT", KC, ev)
                    if j == 1:
                        tr.op("sp", lambda h: h.dma_start(out=p_v[:, :, 1 + t0:1 + t0 + T], in_=st_p[:]),
                              reads=[("st_p", i) for i in range(8)], writes=[("p_s", it)], dma=True)
                return f

            def t_q(j):
                def f(wv, wk):
                    def ev(c, pt, pk):
                        tr.op("act", lambda h: h.activation(out=st_q[:, 4 * j + c, :], in_=pt, func=AF.Copy),
                              reads=[pk], writes=[("st_q", 4 * j + c)])
                    fm_tiles(wv, wk, hT, "hT", KC, ev)
                    if j == 1:
                        tr.op("sp", lambda h: h.dma_start(out=qT_v[:, :, t0:t0 + T], in_=st_q[:]),
                              reads=[("st_q", i) for i in range(8)], writes=[("qT_s", it)], dma=True)
                return f

            def t_k(j):
                def f(wv, wk):
                    def ev(c, pt, pk):
                        tr.op("dve", lambda h: h.tensor_scalar(out=st_k[:, 4 * j + c, :], in0=pt, scalar1=KS,
                                                               scalar2=None, op0=ALU.mult),
                              reads=[pk], writes=[("st_k", 4 * j + c)])
                    fm_tiles(wv, wk, hT, "hT", KC, ev)
                    for tsub in range(4):
                        bk = bank()
                        for kc in range(KC):
                            tr.op("pe", lambda h, kc=kc, bk=bk, tsub=tsub: h.matmul(
                                ps[:, bk, :], lhsT=hT[:, kc, tsub * 128:(tsub + 1) * 128], rhs=wv[:, kc, :],
                                start=(kc == 0), stop=(kc == KC - 1)),
                                reads=[wk, ("hT", kc)], writes=[("ps", bk)])
                        tr.op("act", lambda h, bk=bk, tsub=tsub: h.activation(
                            out=st_kt[:, tsub, 512 * j:512 * j + 512], in_=ps[:, bk, :], func=AF.Identity, scale=KS),
                            reads=[("ps", bk)], writes=[("st_kt", tsub, j)])
                    if j == 1:
                        tr.op("sp", lambda h: h.dma_start(out=kT_v[:, :, t0:t0 + T], in_=st_k[:]),
                              reads=[("st_k", i) for i in range(8)], writes=[("kT_s", it)], dma=True)
                        tr.op("sp", lambda h: h.dma_start(
                            out=k_s[t0:t0 + T, :].rearrange("(s p) e -> p s e", p=128), in_=st_kt[:]),
                            reads=[("st_kt", a, b_) for a in range(4) for b_ in range(2)], writes=[("k_s", it)], dma=True)
                return f

            def t_v(j):
                def f(wv, wk):
                    for tsub in range(4):
                        bk = bank()
                        for kc in range(KC):
                            tr.op("pe", lambda h, kc=kc, bk=bk, tsub=tsub: h.matmul(
                                ps[:, bk, :], lhsT=hT[:, kc, tsub * 128:(tsub + 1) * 128], rhs=wv[:, kc, :],
                                start=(kc == 0), stop=(kc == KC - 1)),
                                reads=[wk, ("hT", kc)], writes=[("ps", bk)])
                        eng = "act" if tsub % 2 == 0 else "dve"
                        if eng == "act":
                            tr.op("act", lambda h, bk=bk, tsub=tsub: h.activation(
                                out=st_vt[:, tsub, 512 * j:512 * j + 512], in_=ps[:, bk, :], func=AF.Copy),
                                reads=[("ps", bk)], writes=[("st_vt", tsub, j)])
                        else:
                            tr.op("dve", lambda h, bk=bk, tsub=tsub: h.tensor_copy(
                                out=st_vt[:, tsub, 512 * j:512 * j + 512], in_=ps[:, bk, :]),
                                reads=[("ps", bk)], writes=[("st_vt", tsub, j)])
                    if j == 3:
                        tr.op("sp", lambda h: h.dma_start(
                            out=v_s[t0:t0 + T, :].rearrange("(s p) e -> p s e", p=128), in_=st_vt[:]),
                            reads=[("st_vt", a, b_) for a in range(4) for b_ in range(4)], writes=[("v_s", it)], dma=True)
                return f

            def t_g(wv, wk):
                for tsub in range(4):
                    bk = bank()
                    for kc in range(KC):
                        tr.op("pe", lambda h, kc=kc, bk=bk, tsub=tsub: h.matmul(
                            ps[:, bk, 0:32], lhsT=hT[:, kc, tsub * 128:(tsub + 1) * 128], rhs=wv[:, kc, :],
                            start=(kc == 0), stop=(kc == KC - 1)),
                            reads=[wk, ("hT", kc)], writes=[("ps", bk)])
                    tr.op("dve", lambda h, bk=bk, tsub=tsub: h.tensor_tensor(
                        out=st_g[:, tsub, :], in0=ps[:, bk, 0:32], in1=s_gbias, op=ALU.add),
                        reads=[("ps", bk), ("gbias",)], writes=[("st_g", tsub)])
                tr.op("sp", lambda h: h.dma_start(
                    out=g_s[t0:t0 + T, :].rearrange("(s p) e -> p s e", p=128), in_=st_g[:]),
                    reads=[("st_g", a) for a in range(4)], writes=[("g_s", it)], dma=True)

            for j in range(2):
                tasks.append((WIN["cc"][j], t_cc(j)))
            for j in range(2):
                tasks.append((WIN["cx"][j], t_cx(j)))
            for j in range(2):
                tasks.append((WIN["q"][j], t_q(j)))
            for j in range(2):
                tasks.append((WIN["k"][j], t_k(j)))
            for j in range(4):
                tasks.append((WIN["v"][j], t_v(j)))
            tasks.append((WG, t_g))
            run_blocks(ws, tasks)

        for it in range(NT):
            tileA(it)
        tr.barrier()

    if "B" in phases:
        NHP = 2
        VW = 264
        NG = NCH * 8
        alB = Alloc(base_persist)
        eb = alB.get("eb", [128, 2, NG], F32)
        er = alB.get("er", [128, 2, NG], F32)
        ed = alB.get("ed", [128, 2, NG], F32)
        ebinv = alB.get("ebinv", [128, 2, NG], F32)
        mnw_sb = alB.get("mnw_sb", [128, D], F32)
        qTh = [alB.get("qTh%d" % i, [128, S], BF16) for i in range(2)]
        kTh = [alB.get("kTh%d" % i, [128, S], BF16) for i in range(2)]
        kth = [alB.get("kth%d" % i, [128, NCH, 128], BF16) for i in range(2)]
        vh = [alB.get("vh%d" % i, [128, NCH, VW], BF16) for i in range(2)]
        hbuf = alB.get("hbuf", [128, 2, NCH, DV], BF16)
        hmT_st = alB.get("hmT_st", [128, 2, 2, S], BF16)
        base_small = alB.off
        alP = Alloc(base_small)
        gsb = alP.get("gsb", [128, NCH, 32], F32)
        nf = alP.get("nf", [128, 2, NG], F32)
        nr = alP.get("nr", [128, 2, NG], F32)
        nfh = alP.get("nfh", [128, 3, 2 * NG], BF16)
        one_t = alP.get("one_t", [128, 8], F32)
        alS = Alloc(base_small)
        Cf = alS.get("Cf", [128, 4, VW], F32)
        Cb = alS.get("Cb", [128, 4, VW], BF16)
        smt = alS.get("smt", [128, 4, 128], BF16)
        vp = alS.get("vp", [128, 4, VW], BF16)
        dn = alS.get("dn", [128, 4, 8], F32)
        ssq = alS.get("ssq", [128, 4, 8], F32)
        hs = alS.get("hs", [128, 4, DV], F32)
        junk = alS.get("junk", [128, 4, DV], BF16)
        hmb = alS.get("hmb", [128, 4, DV], F32)

        g_v = g_s.rearrange("(c p) g -> p c g", p=128)
        for qd in range(4):
            tr.op("sp", lambda h, qd=qd: h.dma_start(out=gsb[:, 8 * qd:8 * qd + 8, :], in_=g_v[:, 8 * qd:8 * qd + 8, :]),
                  writes=[("gsb", qd)], dma=True)
        sp_load(mnw_sb[:], mnw, [("mnw",)])
        tr.op("dve", lambda h: h.memset(one_t[:], 1.0), writes=[("one",)])
        for sl in range(2):
            tr.op("dve", lambda h, sl=sl: h.memset(vh[sl][:, :, DV:DV + 1], 1.0), writes=[("vone", sl)])
        gkeys = [("gsb", qd) for qd in range(4)]
        for d in range(2):
            fcol = 8 + 16 * d
            nfd = nf[:, d, :].rearrange("p (c h) -> p c h", h=8)
            tr.op("act", lambda h, d=d, fcol=fcol, nfd=nfd: h.activation(
                out=nfd, in_=gsb[:, :, fcol:fcol + 8], func=AF.Exp, scale=-1.0),
                reads=gkeys, writes=[("nf", d)])
            tr.op("act", lambda h, d=d: h.activation(
                out=nf[:, d, :], in_=nf[:, d, :], func=AF.Ln, bias=one_t[:, 0:1], scale=1.0),
                reads=[("nf", d), ("one",)], writes=[("nf", d)])
        nf2 = nf[:].rearrange("p d n -> p (d n)")
        nr2 = nr[:].rearrange("p d n -> p (d n)")
        tr.op("dve", lambda h: h.tensor_copy(out=nfh[:, 0, :], in_=nf2), reads=[("nf", 0), ("nf", 1)], writes=[("nfh", 0)])
        tr.op("dve", lambda h: h.tensor_tensor(out=nr2, in0=nf2, in1=nfh[:, 0, :], op=ALU.subtract),
              reads=[("nf", 0), ("nf", 1), ("nfh", 0)], writes=[("nr",)])
        tr.op("dve", lambda h: h.tensor_copy(out=nfh[:, 1, :], in_=nr2), reads=[("nr",)], writes=[("nfh", 1)])
        tr.op("dve", lambda h: h.tensor_tensor(out=nr2, in0=nr2, in1=nfh[:, 1, :], op=ALU.subtract),
              reads=[("nr",), ("nfh", 1)], writes=[("nr",)])
        tr.op("dve", lambda h: h.tensor_copy(out=nfh[:, 2, :], in_=nr2), reads=[("nr",)], writes=[("nfh", 2)])
        for d in range(2):
            tri = triF if d == 0 else triB
            for part in range(3):
                tr.op("pe", lambda h, d=d, part=part, tri=tri: h.matmul(
                    ps[:, d, 0:NG], lhsT=tri, rhs=nfh[:, part, d * NG:(d + 1) * NG], start=(part == 0), stop=(part == 2)),
                    reads=[("nfh", part), ("c_bf",)], excl=[("ps", d)])
            for part in range(3):
                tr.op("pe", lambda h, d=d, part=part: h.matmul(
                    ps[:, 2 + d, 0:NG], lhsT=onesm, rhs=nfh[:, part, d * NG:(d + 1) * NG], start=(part == 0), stop=(part == 2)),
                    reads=[("nfh", part), ("c_bf",)], excl=[("ps", 2 + d)])
            icol = 16 * d
            tr.op("act", lambda h, d=d: h.activation(out=eb[:, d, :], in_=ps[:, d, 0:NG], func=AF.Exp, scale=-1.0),
                  excl=[("ps", d)], writes=[("eb", d)])
            tr.op("act", lambda h, d=d: h.activation(out=ebinv[:, d, :], in_=ps[:, d, 0:NG], func=AF.Exp, scale=1.0),
                  excl=[("ps", d)], writes=[("ebinv", d)])
            tr.op("act", lambda h, d=d: h.activation(out=ed[:, d, :], in_=ps[:, 2 + d, 0:NG], func=AF.Exp, scale=-float(D)),
                  excl=[("ps", 2 + d)], writes=[("ed", d)])
            igc = nr[:, d, :]
            tr.op("act", lambda h, d=d, icol=icol, igc=igc: h.activation(
                out=igc.rearrange("p (c h) -> p c h", h=8), in_=gsb[:, :, icol:icol + 8], func=AF.Exp, scale=1.0),
                reads=gkeys + [("nr",)], writes=[("igc", d)])
            tr.op("dve", lambda h, d=d, igc=igc: h.tensor_tensor(out=er[:, d, :], in0=ebinv[:, d, :], in1=igc, op=ALU.mult),
                  reads=[("ebinv", d), ("igc", d)], writes=[("er", d)])
        tr.barrier()

        wada2 = [alS.get("wada2_%d" % i, [128, KC, 256], BF16) for i in range(2)]
        ps_ada2 = ps[:, 4, 400:464]
        NHB = 32
        ada_state = [0]

        def ada_load(j):
            wt = wada2[j % 2]
            c0 = 4096 + 256 * j
            for qd in range(4):
                tr.op("pool", lambda h, qd=qd: h.dma_start(
                    out=wt[:, 4 * qd:4 * qd + 4, :], in_=w_ada_v[:, 4 * qd:4 * qd + 4, c0:c0 + 256]),
                    writes=[("wada2", j % 2, qd)], dma=True)

        def ada_mm(j):
            wt = wada2[j % 2]
            for sub in range(2):
                col = 2 * j + sub
                for kc in range(KC):
                    tr.op("pe", lambda h, sub=sub, kc=kc, col=col: h.matmul(
                        ps_ada2[:, col:col + 1], lhsT=wt[:, kc, sub * 128:(sub + 1) * 128], rhs=cact[:, kc:kc + 1],
                        start=(kc == 0), stop=(kc == KC - 1)),
                        reads=[("wada2", j % 2, kc // 4), ("cact",)], excl=[("ps", 4)])

        conv_q = [b for b in wblocks if not b.converted]

        def conv_some(n):
            for _ in range(n):
                if conv_q:
                    convert(conv_q.pop(0))

        ada_load(0)
        ada_load(1)
        conv_some(4)

        k_v = k_s.rearrange("(c p) e -> p c e", p=128)
        v_v = v_s.rearrange("(c p) e -> p c e", p=128)
        hm_v = hm_s.rearrange("(v p) t -> p v t", p=128)

        def load_head(hd):
            sl = hd % 2
            tr.op("sp", lambda h: h.dma_start(out=qTh[sl][:], in_=qT_s[hd * 128:(hd + 1) * 128, :]),
                  writes=[("qTh", sl)], dma=True)
            tr.op("sp", lambda h: h.dma_start(out=kTh[sl][:], in_=kT_s[hd * 128:(hd + 1) * 128, :]),
                  writes=[("kTh", sl)], dma=True)
            for qd in range(4):
                tr.op("sp", lambda h, qd=qd: h.dma_start(
                    out=kth[sl][:, 8 * qd:8 * qd + 8, :], in_=k_v[:, 8 * qd:8 * qd + 8, hd * 128:(hd + 1) * 128]),
                    writes=[("kth", sl, qd)], dma=True)
                tr.op("sp", lambda h, qd=qd: h.dma_start(
                    out=vh[sl][:, 8 * qd:8 * qd + 8, 0:DV], in_=v_v[:, 8 * qd:8 * qd + 8, hd * DV:(hd + 1) * DV]),
                    reads=[("vone", sl)], writes=[("vh", sl, qd)], dma=True)

        KVB = {0: 4, 1: 5, 2: 6, 3: 7}

        EST = {"pe": 0.28, "act": 0.48, "dve": 0.42}

        def dop(eng, fn, t=None, **kw):
            return (eng, lambda: tr.op(eng, fn, **kw), EST[eng] if t is None else t)

        def step(hd, i, d):
            sl = hd % 2
            ch = 2 * sl + d
            c = i if d == 0 else NCH - 1 - i
            cprev = c - 1 if d == 0 else c + 1
            col, colp = c * 8 + hd, cprev * 8 + hd
            tok = slice(c * 128, (c + 1) * 128)
            qd = c // 8
            bn, bkv = ("ps", ch), ("ps", KVB[ch])
            psn, pss, pskv = ps[:, ch, 0:DV + 1], ps[:, ch, 260:388], ps[:, KVB[ch], 0:DV + 1]
            fin = i >= NCH // 2
            if fin:
                yield dop("dve", lambda h: h.memset(ssq[:, ch, 0:1], 0.0), t=0.05, writes=[("ssq", ch)])
            yield dop("pe", lambda h: h.matmul(pss, lhsT=kTh[sl][:, tok], rhs=qTh[sl][:, tok], start=True, stop=True),
                        reads=[("kTh", sl), ("qTh", sl)], excl=[bn])
            yield dop("dve", lambda h: h.tensor_tensor(out=smt[:, ch, :], in0=pss, in1=c_mat[:, d, :], op=ALU.mult),
                        reads=[("c_mat",)], excl=[bn], writes=[("smt", ch)])
            yield dop("act", lambda h: h.activation(out=vp[:, ch, 0:DV + 1], in_=vh[sl][:, c, 0:DV + 1], func=AF.Identity,
                                                      scale=er[:, d, col:col + 1]),
                        reads=[("vh", sl, qd), ("vone", sl), ("er", d)], writes=[("vp", ch)])
            yield dop("pe", lambda h: h.matmul(psn, lhsT=smt[:, ch, :], rhs=vp[:, ch, 0:DV + 1], start=True, stop=(i == 0)),
                        reads=[("smt", ch), ("vp", ch)], excl=[bn])
            if i > 0:
                yield dop("pe", lambda h: h.matmul(psn, lhsT=qTh[sl][:, tok], rhs=Cb[:, ch, 0:DV + 1], start=False, stop=True),
                            reads=[("qTh", sl), ("Cb", ch)], excl=[bn])
            yield dop("pe", lambda h: h.matmul(pskv, lhsT=kth[sl][:, c, :], rhs=vp[:, ch, 0:DV + 1], start=True, stop=True),
                        reads=[("kth", sl, qd), ("vp", ch)], excl=[bkv])
            if i == 0:
                yield dop("dve", lambda h: h.tensor_copy(out=Cf[:, ch, 0:DV + 1], in_=pskv), excl=[bkv], writes=[("Cf", ch)])
            else:
                yield dop("dve", lambda h: h.scalar_tensor_tensor(
                    out=Cf[:, ch, 0:DV + 1], in0=Cf[:, ch, 0:DV + 1], scalar=ed[:, d, colp:colp + 1], in1=pskv,
                    op0=ALU.mult, op1=ALU.add),
                    reads=[("Cf", ch), ("ed", d)], excl=[bkv], writes=[("Cf", ch)])
            if i < NCH - 1:
                yield dop("act", lambda h: h.activation(out=Cb[:, ch, 0:DV + 1], in_=Cf[:, ch, 0:DV + 1], func=AF.Identity,
                                                          scale=ed[:, d, col:col + 1]),
                            reads=[("Cf", ch), ("ed", d)], writes=[("Cb", ch)])
            ebi = ebinv[:, d, col:col + 1]
            yield dop("dve", lambda h: h.tensor_scalar(out=dn[:, ch, 0:1], in0=psn[:, DV:DV + 1], scalar1=-1.0, scalar2=ebi,
                                                         op0=ALU.mult, op1=ALU.max),
                        t=0.16, reads=[("ebinv", d)], excl=[bn], writes=[("dn", ch)])
            yield dop("dve", lambda h: h.tensor_tensor(out=dn[:, ch, 1:2], in0=psn[:, DV:DV + 1], in1=dn[:, ch, 0:1], op=ALU.max),
                        t=0.16, reads=[("dn", ch)], excl=[bn], writes=[("dn1", ch)])
            yield dop("dve", lambda h: h.reciprocal(out=dn[:, ch, 4:5], in_=dn[:, ch, 1:2]), t=0.16, reads=[("dn1", ch)], writes=[("fsc", ch)])
            fsc = dn[:, ch, 4:5]
            if not fin:
                yield dop("act", lambda h: h.activation(out=hbuf[:, sl, c, :], in_=psn[:, 0:DV], func=AF.Identity, scale=fsc),
                            reads=[("fsc", ch)], excl=[bn], writes=[("hbuf", sl, c)])
            else:
                yield dop("dve", lambda h: h.scalar_tensor_tensor(out=hs[:, ch, :], in0=psn[:, 0:DV], scalar=fsc,
                                                                    in1=hbuf[:, sl, c, :], op0=ALU.mult, op1=ALU.add),
                            reads=[("fsc", ch), ("hbuf", sl, c)], excl=[bn], writes=[("hs", ch)])
                yield dop("act", lambda h: h.activation(out=junk[:, ch, :], in_=hs[:, ch, :], func=AF.Square,
                                                          accum_out=ssq[:, ch, 0:1]),
                            reads=[("hs", ch), ("ssq", ch)], writes=[("ssq", ch), ("junk", ch)])
                yield dop("act", lambda h: h.activation(out=ssq[:, ch, 1:2], in_=ssq[:, ch, 0:1], func=AF.Sqrt,
                                                          bias=s_eps, scale=1.0 / DV),
                            reads=[("ssq", ch), ("eps",)], writes=[("ssq1", ch)])
                yield dop("dve", lambda h: h.reciprocal(out=ssq[:, ch, 2:3], in_=ssq[:, ch, 1:2]),
                            t=0.16, reads=[("ssq1", ch)], writes=[("ssq2", ch)])
                yield dop("dve", lambda h: h.scalar_tensor_tensor(
                    out=hmb[:, ch, :], in0=hs[:, ch, :], scalar=ssq[:, ch, 2:3], in1=mnw_sb[:, hd * DV:(hd + 1) * DV],
                    op0=ALU.mult, op1=ALU.mult),
                    reads=[("hs", ch), ("ssq2", ch), ("mnw",)], writes=[("hmb", ch)])
                tb = [bn, bkv]
                tps = [ps[:, ch, 260:388], ps[:, KVB[ch], 260:388]]
                for vhf in range(2):
                    yield dop("pe", lambda h, vhf=vhf: h.transpose(
                        out=tps[vhf], in_=hmb[:, ch, vhf * 128:(vhf + 1) * 128], identity=c_mat[:, 3, :]),
                        reads=[("hmb", ch), ("c_mat",)], excl=[tb[vhf]])
                for vhf in range(2):
                    yield dop("act", lambda h, vhf=vhf: h.activation(
                        out=hmT_st[:, sl, vhf, tok], in_=tps[vhf], func=AF.Copy),
                        excl=[tb[vhf]], writes=[("hmT_st", sl, c, vhf)])

        for hp in range(H // NHP):
            heads = [hp * NHP + j for j in range(NHP)]
            for hd in heads:
                load_head(hd)
            def chain(hd, d):
                for i in range(NCH):
                    yield from step(hd, i, d)
                    yield ("end", None, 0.0)

            free = {"pe": 0.0, "act": 0.0, "dve": 0.0}
            chs = []
            for k, (hd, d) in enumerate((hd, d) for hd in heads for d in range(2)):
                g = chain(hd, d)
                chs.append({"g": g, "it": next(g), "rdy": 1.7 * k, "done": 0, "hd": hd, "alive": True})
            emitted = 0
            while any(c["alive"] for c in chs):
                best, bst = None, None
                for cobj in chs:
                    if not cobj["alive"]:
                        continue
                    if cobj["it"][0] == "end":
                        partner = [o for o in chs if o["hd"] == cobj["hd"] and o is not cobj][0]
                        ok = (not partner["alive"]) or partner["done"] >= cobj["done"] + 1 or \
                            (partner["it"][0] == "end" and partner["done"] == cobj["done"])
                        if not ok:
                            continue
                        st = cobj["rdy"]
                    else:
                        st = max(free[cobj["it"][0]], cobj["rdy"])
                    if bst is None or st < bst:
                        best, bst = cobj, st
                eng, thunk, dur = best["it"]
                if eng == "end":
                    best["done"] += 1
                else:
                    thunk()
                    emitted += 1
                    free[eng] = bst + dur
                    best["rdy"] = bst + dur + 0.12
                    if emitted % 240 == 0 and ada_state[0] < NHB:
                        j = ada_state[0]
                        ada_mm(j)
                        if j + 2 < NHB:
                            ada_load(j + 2)
                        conv_some(1)
                        ada_state[0] += 1
                try:
                    best["it"] = next(best["g"])
                except StopIteration:
                    best["alive"] = False
            for hd in heads:
                tr.op("sp", lambda h, hd=hd: h.dma_start(out=hm_v[:, 2 * hd:2 * hd + 2, :], in_=hmT_st[:, hd % 2]),
                      reads=[("hmT_st", hd % 2, c, v_) for c in range(NCH) for v_ in range(2)], writes=[("hm_s", hd)], dma=True)
        while ada_state[0] < NHB:
            j = ada_state[0]
            ada_mm(j)
            if j + 2 < NHB:
                ada_load(j + 2)
            ada_state[0] += 1
        tr.op("dve", lambda h: h.tensor_tensor(out=s_ada[:, 32:96], in0=ps_ada2, in1=s_bada[:, 32:96], op=ALU.add),
              reads=[("bada",)], excl=[("ps", 4)], writes=[("ada2",)])
        tr.op("dve", lambda h: h.scalar_tensor_tensor(out=s_a2, in0=s_ada[:, 64:80], scalar=1.0, in1=s_n2w,
                                                       op0=ALU.add, op1=ALU.mult),
              reads=[("ada2",), ("n2w",)], writes=[("a2",)])
        if dbg:
            tr.op("sp", lambda h: h.dma_start(out=ada_dbg, in_=s_ada), reads=[("ada2",)], writes=[("ada_dbg",)], dma=True)
        tr.barrier()

    if "C" in phases:
        PC_BLOCKS = WIN["cb"] + WIN["o"]
        for j in range(4):
            PC_BLOCKS += [WIN["bgc"][j], WCO[j], WIN["bgm"][j], WMO[j]]
        PC_BLOCKS += WOO
        for j in range(11):
            PC_BLOCKS += [WGG[j], WGU[j]]
        for j in range(4):
            PC_BLOCKS += WDN[j]
        for b in wblocks:
            convert(b)
        alC = Alloc(base_persist)
        xs = alC.get("xsC", [128, KC, T], F32)
        hT = alC.get("hTC", [128, KC, T], BF16)
        actT = alC.get("actT", [128, FKC, T], BF16)
        hmT = alC.get("hmT", [128, KC, T], BF16)
        pl = alC.get("pl", [128, 8, T + 2], F32)
        cvt = alC.get("cvt", [128, 2, T], F32)
        rstd = alC.get("rstdC", [128, T], F32)
        ntmp = alC.get("ntmp", [128, 2, T], F32)
        G1 = alC.get("G1", [128, 4, T], F32)
        mg = alC.get("mg", [128, 4, T], F32)
        tmpF = alC.get("tmpF", [128, 2, T], F32)
        ws = WStream(alC)
        merged = actT[:, 0:16, :]
        uT = actT[:, 16:24, :]
        sq = actT[:, 24:40, :]
        xT_v = xT.rearrange("(kc p) t -> p kc t", p=128)
        oT_v = outT.rearrange("(kc p) t -> p kc t", p=128)
        hmr_v = hm_s.rearrange("(kc p) t -> p kc t", p=128)
        tf = [0]
        cvn = [0]

        def tileC(it):
            t0 = it * T
            for half in range(2):
                tr.op("sp", lambda h, half=half: h.dma_start(
                    out=xs[:, 8 * half:8 * half + 8, :], in_=xT_v[:, 8 * half:8 * half + 8, t0:t0 + T]),
                    writes=[("xs", kc) for kc in range(8 * half, 8 * half + 8)], dma=True)
            tr.op("sp", lambda h: h.dma_start(out=pl[:], in_=p_v[:, :, t0:t0 + T + 2]), writes=[("pl", ch) for ch in range(8)], dma=True)
            tr.op("sp", lambda h: h.dma_start(out=hmT[:], in_=hmr_v[:, :, t0:t0 + T]), writes=[("hmT", kc) for kc in range(KC)], dma=True)
            sqk = lambda kc: ("act", 24 + kc)
            norm_stats(xs, sq, rstd, "xs", sqk)
            norm_apply(xs, "xs", rstd, s_a1, ("a1",), b1, ("ada",), hT, ntmp, False)
            tasks = []

            def conv_chunk(ch):
                slot = cvn[0] % 2
                cvn[0] += 1
                cv = cvt[:, slot, :]
                tr.op("dve", lambda h: h.tensor_scalar(out=cv, in0=pl[:, ch, 0:T], scalar1=s_convw[:, ch:ch + 1],
                                                        scalar2=None, op0=ALU.mult),
                      reads=[("pl", ch), ("convw",)], writes=[("cvt", slot)])
                for tap in (1, 2):
                    tr.op("dve", lambda h, tap=tap: h.scalar_tensor_tensor(
                        out=cv, in0=pl[:, ch, tap:tap + T], scalar=s_convw[:, 8 * tap + ch:8 * tap + ch + 1], in1=cv,
                        op0=ALU.mult, op1=ALU.add),
                        reads=[("pl", ch), ("convw",), ("cvt", slot)], writes=[("cvt", slot)])
                return cv, ("cvt", slot)

            def t_cb(j):
                def f(wv, wk):
                    def ev(c, pt, pk):
                        cv, ck = conv_chunk(4 * j + c)
                        tr.op("dve", lambda h: h.tensor_tensor(out=uT[:, 4 * j + c, :], in0=pt, in1=cv, op=ALU.mult),
                              reads=[pk, ck], writes=[("act", 16 + 4 * j + c)])
                    fm_tiles(wv, wk, hT, "hT", KC, ev)
                return f

            def t_o(j):
                def f(wv, wk):
                    def ev(c, pt, pk):
                        slot = tf[0] % 2
                        tf[0] += 1
                        tr.op("act", lambda h: h.activation(out=tmpF[:, slot, :], in_=pt, func=AF.Sigmoid),
                              reads=[pk], writes=[("tmpF", slot)])
                        tr.op("dve", lambda h: h.tensor_tensor(out=hmT[:, 4 * j + c, :], in0=tmpF[:, slot, :],
                                                               in1=hmT[:, 4 * j + c, :], op=ALU.mult),
                              reads=[("tmpF", slot), ("hmT", 4 * j + c)], writes=[("hmT", 4 * j + c)])
                    fm_tiles(wv, wk, hT, "hT", KC, ev)
                return f

            def t_bg(j):
                def f(wv, wk):
                    def ev(c, pt, pk):
                        tr.op("act", lambda h: h.activation(out=G1[:, c, :], in_=pt, func=AF.Sigmoid),
                              reads=[pk], writes=[("G1", c)])
                    fm_tiles(wv, wk, hT, "hT", KC, ev)
                return f

            def t_co(j):
                def f(wv, wk):
                    def ev(c, pt, pk):
                        tr.op("dve", lambda h: h.tensor_tensor(out=mg[:, c, :], in0=pt, in1=G1[:, c, :], op=ALU.mult),
                              reads=[pk, ("G1", c)], writes=[("mg", c)])
                    fm_tiles(wv, wk, uT, lambda kc: ("act", 16 + kc), 8, ev)
                return f

            def t_mo(j):
                def f(wv, wk):
                    def ev(c, pt, pk):
                        slot = tf[0] % 2
                        tf[0] += 1
                        tr.op("dve", lambda h: h.tensor_tensor(out=tmpF[:, slot, :], in0=pt, in1=G1[:, c, :], op=ALU.mult),
                              reads=[pk, ("G1", c)], writes=[("tmpF", slot)])
                        tr.op("pool", lambda h: h.tensor_tensor(out=merged[:, 4 * j + c, :], in0=tmpF[:, slot, :],
                                                                in1=mg[:, c, :], op=ALU.add),
                              reads=[("tmpF", slot), ("mg", c)], writes=[("act", 4 * j + c)])
                    fm_tiles(wv, wk, hmT, "hmT", KC, ev)
                return f

            def t_res(j, act_t, akey, gcol):
                def f(wv, wk):
                    def ev(c, pt, pk):
                        e = 4 * j + c
                        tr.op("dve", lambda h: h.scalar_tensor_tensor(
                            out=xs[:, e, :], in0=pt, scalar=gcol[:, e:e + 1], in1=xs[:, e, :], op0=ALU.mult, op1=ALU.add),
                            reads=[pk, ("xs", e), ("ada",)], writes=[("xs", e)])
                    fm_tiles(wv, wk, act_t, akey, KC, ev)
                return f

            tasks += [(WIN["cb"][j], t_cb(j)) for j in range(2)]
            tasks += [(WIN["o"][j], t_o(j)) for j in range(4)]
            for j in range(4):
                tasks += [(WIN["bgc"][j], t_bg(j)), (WCO[j], t_co(j)), (WIN["bgm"][j], t_bg(j)), (WMO[j], t_mo(j))]
            tasks += [(WOO[j], t_res(j, merged, lambda kc: ("act", kc), g1)) for j in range(4)]
            run_blocks(ws, tasks)

            norm_stats(xs, sq, rstd, "xs", sqk)
            norm_apply(xs, "xs", rstd, s_a2, ("a2",), b2, ("ada",), hT, ntmp, False)
            tasks = []

            def t_gg(j):
                def f(wv, wk):
                    def ev(c, pt, pk):
                        tr.op("act", lambda h: h.activation(out=G1[:, c, :], in_=pt, func=AF.Silu),
                              reads=[pk], writes=[("G1", c)])
                    fm_tiles(wv, wk, hT, "hT", KC, ev)
                return f

            def t_gu(j):
                def f(wv, wk):
                    def ev(c, pt, pk):
                        tr.op("dve", lambda h: h.tensor_tensor(out=actT[:, 4 * j + c, :], in0=pt, in1=G1[:, c, :], op=ALU.mult),
                              reads=[pk, ("G1", c)], writes=[("act", 4 * j + c)])
                    fm_tiles(wv, wk, hT, "hT", KC, ev)
                return f

            dn_banks = {}

            def t_dn(j, q):
                def f(wv, wk):
                    if q == 0:
                        dn_banks[j] = [bank() for _ in range(4)]
                    for c in range(4):
                        bk = dn_banks[j][c]
                        for kc in range(11):
                            tr.op("pe", lambda h, c=c, kc=kc, bk=bk: h.matmul(
                                ps[:, bk, :], lhsT=wv[:, kc, c * 128:(c + 1) * 128], rhs=actT[:, 11 * q + kc, :],
                                start=(q == 0 and kc == 0), stop=(q == 3 and kc == 10)),
                                reads=[wk, ("act", 11 * q + kc)], writes=[("ps", bk)])
                        if q == 3:
                            e = 4 * j + c
                            tr.op("dve", lambda h, e=e, bk=bk: h.scalar_tensor_tensor(
                                out=xs[:, e, :], in0=ps[:, bk, :], scalar=g2[:, e:e + 1], in1=xs[:, e, :],
                                op0=ALU.mult, op1=ALU.add),
                                reads=[("ps", bk), ("xs", e), ("ada",)], writes=[("xs", e)])
                return f

            for j in range(11):
                tasks += [(WGG[j], t_gg(j)), (WGU[j], t_gu(j))]
            for j in range(4):
                tasks += [(WDN[j][q], t_dn(j, q)) for q in range(4)]
            run_blocks(ws, tasks)

            norm_stats(xs, sq, rstd, "xs", sqk)
            for kc in range(KC):
                tr.op("dve", lambda h, kc=kc: h.scalar_tensor_tensor(
                    out=xs[:, kc, :], in0=xs[:, kc, :], scalar=s_fnw[:, kc:kc + 1], in1=rstd[:], op0=ALU.mult, op1=ALU.mult),
                    reads=[("xs", kc), ("rstd",), ("fnw",)], writes=[("xs", kc)])
            for half in range(2):
                tr.op("sp", lambda h, half=half: h.dma_start(
                    out=oT_v[:, 8 * half:8 * half + 8, t0:t0 + T], in_=xs[:, 8 * half:8 * half + 8, :]),
                    reads=[("xs", kc) for kc in range(8 * half, 8 * half + 8)], writes=[("out", it, half)], dma=True)

        for it in range(NT):
            tileC(it)

    with nc.Block() as block:
        tr.emit(block)
    return nc


def make_inputs(inputs, b):
    f = np.float32
    x = np.asarray(inputs["x"], f)
    d = {}
    d["xT"] = np.ascontiguousarray(x[b].T)
    d["ccol"] = np.ascontiguousarray(np.asarray(inputs["c"], f)[b].reshape(KC, 128).T)
    d["w_ada"] = np.ascontiguousarray(np.asarray(inputs["w_ada"], f)[0])
    d["bada"] = np.ascontiguousarray(np.asarray(inputs["b_ada"], f)[0].reshape(96, 128).T)
    d["n1w"] = np.ascontiguousarray(np.asarray(inputs["norm1_w"], f)[0].reshape(KC, 128).T)
    d["n2w"] = np.ascontiguousarray(np.asarray(inputs["norm2_w"], f)[0].reshape(KC, 128).T)
    d["fnw"] = np.ascontiguousarray(np.asarray(inputs["final_norm_w"], f).reshape(KC, 128).T)
    d["w_in"] = np.ascontiguousarray(np.asarray(inputs["w_in_mix"], f)[0])
    d["convw"] = np.ascontiguousarray(np.asarray(inputs["conv_w"], f)[0].reshape(3, 8, 128).transpose(2, 0, 1).reshape(128, 24))
    d["gbias"] = np.ascontiguousarray(np.broadcast_to(np.asarray(inputs["mlstm_gate_bias"], f)[0][None, :], (128, 32)))
    d["mnw"] = np.ascontiguousarray(np.broadcast_to(np.asarray(inputs["mlstm_norm_w"], f)[0][None, :], (128, D)))
    d["w_co"] = np.ascontiguousarray(np.asarray(inputs["w_conv_out"], f)[0])
    d["w_mo"] = np.ascontiguousarray(np.asarray(inputs["w_mlstm_out"], f)[0])
    d["w_oo"] = np.ascontiguousarray(np.asarray(inputs["w_o"], f)[0])
    d["w_gu"] = np.ascontiguousarray(np.asarray(inputs["w_gate_up"], f)[0])
    d["w_dn"] = np.ascontiguousarray(np.asarray(inputs["w_down"], f)[0])
    cm = np.zeros((128, 4, 128), f)
    i = np.arange(128)
    cm[:, 0, :] = (i[:, None] <= i[None, :])
    cm[:, 1, :] = (i[:, None] >= i[None, :])
    cm[:, 2, :] = 1.0 / D
    cm[:, 3, :] = np.eye(128)
    d["cmat"] = cm
    return d


_NC_CACHE = {}


def kernel(**inputs):
    if "nc" not in _NC_CACHE:
        _NC_CACHE["nc"] = build_program()
    nc = _NC_CACHE["nc"]
    in_maps = [make_inputs(inputs, b) for b in range(NCORES)]
    res = run_bass_kernel_spmd(nc, in_maps, core_ids=list(range(NCORES)))
    out = np.stack([np.ascontiguousarray(res.results[b]["outT"].T) for b in range(NCORES)], axis=0)
    return out.astype(np.float32)
```
